# Optimizing a Trainium2 kernel written in Bass

```python
import functools
import jax, jax.numpy as jnp
from jax import lax
import numpy as np

D_MODEL = 1024
BATCH = 16
SEQ = 256
DEPTH = 1
DEC_BATCH = 2
DEC_SEQ = 4096
PAST_LEN = 256

GRID_W = 64
FOURIER_WIDTH = 512
FOURIER_GROUPS = 4
FOURIER_GROUP_DIM = FOURIER_WIDTH // FOURIER_GROUPS
RWKV_WIDTH = 1024
HEAD_DIM = 64
N_HEADS = RWKV_WIDTH // HEAD_DIM
DECAY_RANK = 64
AAA_RANK = 64
N_DIR = 2
N_BRANCH = 2
SHIFT_WIDTH = 3 * RWKV_WIDTH + DECAY_RANK + AAA_RANK
IN_WIDTH = 2 * FOURIER_WIDTH + SHIFT_WIDTH + RWKV_WIDTH + N_BRANCH * D_MODEL
RMS_EPS = 1e-6
GN_EPS = 64e-5

kernel_name = "hybrid_fnet_rwkv7_flow_step"


def _rmsnorm(x, g):
    xf = x.astype(jnp.float32)
    y = xf * lax.rsqrt(jnp.mean(xf * xf, axis=-1, keepdims=True) + RMS_EPS)
    return (y * g.astype(jnp.float32)).astype(x.dtype)


def _shift_context(u):
    B, T, C = u.shape
    v = u.reshape(B, T, C // 2, 2)
    prev = jnp.pad(v[:, :-1, :, 0], ((0, 0), (1, 0), (0, 0)))
    nxt = jnp.pad(v[:, 1:, :, 1], ((0, 0), (0, 1), (0, 0)))
    return jnp.stack([prev, nxt], axis=-1).reshape(B, T, C)


def _shift_grid(u, rows):
    B, T, C = u.shape
    v = u.reshape(B, rows, GRID_W, C // 4, 4)
    left = jnp.pad(v[:, :, :-1, :, 0], ((0, 0), (0, 0), (1, 0), (0, 0)))
    right = jnp.pad(v[:, :, 1:, :, 1], ((0, 0), (0, 0), (0, 1), (0, 0)))
    up = jnp.pad(v[:, :-1, :, :, 2], ((0, 0), (1, 0), (0, 0), (0, 0)))
    down = jnp.pad(v[:, 1:, :, :, 3], ((0, 0), (0, 1), (0, 0), (0, 0)))
    return jnp.stack([left, right, up, down], axis=-1).reshape(B, T, C)


def _fourier_mix(u):
    B, T, _ = u.shape
    ug = u.astype(jnp.float32).reshape(B, T, FOURIER_GROUPS, FOURIER_GROUP_DIM)
    f = jnp.fft.fftn(ug, axes=(1, 3), norm="ortho").real
    return f.reshape(B, T, FOURIER_WIDTH).astype(u.dtype)


def _rwkv7_scan(S0, r, w, k, v, a_vec, b_vec):
    def step(S, inp):
        r_t, w_t, k_t, v_t, a_t, b_t = inp
        sa = jnp.einsum('bhij,bhj->bhi', S, a_t)
        S = S * w_t[:, :, None, :] + sa[..., None] * b_t[:, :, None, :] + v_t[..., None] * k_t[:, :, None, :]
        y = jnp.einsum('bhij,bhj->bhi', S, r_t)
        return S, y
    xs = tuple(jnp.moveaxis(t, 1, 0) for t in (r, w, k, v, a_vec, b_vec))
    S_fin, ys = lax.scan(step, S0.astype(jnp.float32), xs)
    return S_fin, jnp.moveaxis(ys, 0, 1)


def _mixer(h, S0, shift_fn, w_in, mu_shift, w0, w_up, a0, a_up, k_k, k_a, r_k,
           lnx_g, lnx_b, w_proj_f, w_proj_r, w_out):
    B, T, _ = h.shape
    f32 = jnp.float32
    u = h @ w_in
    offs = [FOURIER_WIDTH, 2 * FOURIER_WIDTH, 2 * FOURIER_WIDTH + SHIFT_WIDTH,
            2 * FOURIER_WIDTH + SHIFT_WIDTH + RWKV_WIDTH]
    xf, gf, sh, gr, mg = jnp.split(u, offs, axis=-1)

    out_f = (_fourier_mix(xf) * jax.nn.silu(gf)) @ w_proj_f

    sh = sh + mu_shift * (shift_fn(sh) - sh)
    r, k, v, wd, ad = jnp.split(
        sh, [RWKV_WIDTH, 2 * RWKV_WIDTH, 3 * RWKV_WIDTH, 3 * RWKV_WIDTH + DECAY_RANK], axis=-1)
    heads = lambda t: t.astype(f32).reshape(t.shape[:-1] + (N_HEADS, HEAD_DIM))
    wl = (jnp.einsum('btr,zrc->zbtc', jnp.tanh(wd), w_up) + w0[:, None, None, :]).astype(f32)
    decay = heads(jnp.exp(-jnp.exp(-jax.nn.softplus(-wl) - 0.5)))
    a = heads(jax.nn.sigmoid(jnp.einsum('btr,zrc->zbtc', ad, a_up) + a0[:, None, None, :]))
    rh, kh, vh = heads(r), heads(k), heads(v)
    kk = heads(k * k_k)
    kk = kk / jnp.maximum(jnp.sqrt(jnp.sum(kk * kk, axis=-1, keepdims=True)), 1e-12)
    k_dir = kh[None] * (1.0 + (a - 1.0) * k_a.astype(f32).reshape(N_HEADS, HEAD_DIM))
    b_dir = kk[None] * a
    S_f, y_f = _rwkv7_scan(S0[:, 0], rh, decay[0], k_dir[0], vh, -kk, b_dir[0])
    flip = lambda t: jnp.flip(t, axis=1)
    S_b, y_b = _rwkv7_scan(S0[:, 1], flip(rh), flip(decay[1]), flip(k_dir[1]), flip(vh),
                           flip(-kk), flip(b_dir[1]))
    y = y_f + flip(y_b)
    mean = jnp.mean(y, axis=-1, keepdims=True)
    var = jnp.mean((y - mean) ** 2, axis=-1, keepdims=True)
    yn = ((y - mean) * lax.rsqrt(var + GN_EPS)).reshape(B, T, RWKV_WIDTH)
    yn = yn * lnx_g.astype(f32) + lnx_b.astype(f32)
    bonus = (jnp.sum(rh * kh * r_k.astype(f32), axis=-1, keepdims=True) * vh).reshape(B, T, RWKV_WIDTH)
    out_r = ((yn + bonus).astype(h.dtype) * jax.nn.silu(gr)) @ w_proj_r

    gates = jax.nn.sigmoid(mg.reshape(B, T, N_BRANCH, D_MODEL))
    merged = gates[..., 0, :] * out_f + gates[..., 1, :] * out_r
    return merged @ w_out, jnp.stack([S_f, S_b], axis=1)


def setup_inputs(seed: int = 0) -> dict:
    key = jax.random.key(seed)
    ks = jax.random.split(key, 24)
    nrm = lambda k, s, sc: jax.random.normal(k, s, jnp.float32) * sc
    D = D_MODEL
    return {
        "x_prompt": nrm(ks[0], (BATCH, SEQ, D), 1.0),
        "x_sample": nrm(ks[1], (DEC_BATCH, DEC_SEQ, D), 1.0),
        "state_rwkv": nrm(ks[2], (DEC_BATCH, DEPTH, N_DIR, N_HEADS, HEAD_DIM, HEAD_DIM), 0.3),
        "c": nrm(ks[3], (DEC_BATCH, D), 1.0),
        "c_ctx": nrm(ks[4], (D,), 1.0),
        "norm_g": 1.0 + nrm(ks[5], (DEPTH, D), 0.01),
        "w_ada": nrm(ks[6], (DEPTH, D, 3 * D), 0.5 * D ** -0.5),
        "b_ada": nrm(ks[7], (DEPTH, 3 * D), 0.01),
        "w_in": nrm(ks[8], (DEPTH, D, IN_WIDTH), D ** -0.5),
        "mu_shift": jax.random.uniform(ks[9], (DEPTH, SHIFT_WIDTH), jnp.float32),
        "w0": jax.random.uniform(ks[10], (DEPTH, N_DIR, RWKV_WIDTH), jnp.float32, -6.0, 1.0),
        "w_up": nrm(ks[11], (DEPTH, N_DIR, DECAY_RANK, RWKV_WIDTH), 0.1 * DECAY_RANK ** -0.5),
        "a0": nrm(ks[12], (DEPTH, N_DIR, RWKV_WIDTH), 0.5),
        "a_up": nrm(ks[13], (DEPTH, N_DIR, AAA_RANK, RWKV_WIDTH), 0.1 * AAA_RANK ** -0.5),
        "k_k": 0.85 + nrm(ks[14], (DEPTH, RWKV_WIDTH), 0.05),
        "k_a": 1.0 + nrm(ks[15], (DEPTH, RWKV_WIDTH), 0.05),
        "r_k": nrm(ks[16], (DEPTH, N_HEADS, HEAD_DIM), 0.1),
        "lnx_g": 1.0 + nrm(ks[17], (DEPTH, RWKV_WIDTH), 0.01),
        "lnx_b": nrm(ks[18], (DEPTH, RWKV_WIDTH), 0.01),
        "w_proj_f": nrm(ks[19], (DEPTH, FOURIER_WIDTH, D), FOURIER_WIDTH ** -0.5),
        "w_proj_r": nrm(ks[20], (DEPTH, RWKV_WIDTH, D), RWKV_WIDTH ** -0.5),
        "w_out": nrm(ks[21], (DEPTH, D, D), D ** -0.5),
        "final_g": 1.0 + nrm(ks[22], (D,), 0.01),
    }


def reference(x_prompt, x_sample, state_rwkv, c, c_ctx, norm_g, w_ada, b_ada, w_in, mu_shift,
              w0, w_up, a0, a_up, k_k, k_a, r_k, lnx_g, lnx_b, w_proj_f, w_proj_r, w_out, final_g):
    rows = x_sample.shape[1] // GRID_W
    grid_shift = functools.partial(_shift_grid, rows=rows)
    xp, xs = x_prompt, x_sample
    S_ctx0 = jnp.zeros((xp.shape[0], N_DIR, N_HEADS, HEAD_DIM, HEAD_DIM), jnp.float32)
    new_states = []
    for l in range(DEPTH):
        params = (w_in[l], mu_shift[l], w0[l], w_up[l], a0[l], a_up[l], k_k[l], k_a[l], r_k[l],
                  lnx_g[l], lnx_b[l], w_proj_f[l], w_proj_r[l], w_out[l])
        m_ctx = jax.nn.silu(c_ctx) @ w_ada[l] + b_ada[l]
        sft, scl, gte = jnp.split(m_ctx, 3, axis=-1)
        hp = _rmsnorm(xp, norm_g[l]) * (1.0 + scl) + sft
        op, S_ctx = _mixer(hp, S_ctx0, _shift_context, *params)
        xp = xp + gte * op
        new_states.append(S_ctx)
        m_lat = jax.nn.silu(c) @ w_ada[l] + b_ada[l]
        sft_s, scl_s, gte_s = jnp.split(m_lat[:, None, :], 3, axis=-1)
        hs = _rmsnorm(xs, norm_g[l]) * (1.0 + scl_s) + sft_s
        os_, _ = _mixer(hs, state_rwkv[:, l], grid_shift, *params)
        xs = xs + gte_s * os_
    y_prompt = _rmsnorm(xp, final_g)
    y_sample = _rmsnorm(xs, final_g)
    new_state_rwkv = jnp.stack(new_states, axis=1)
    return (y_prompt, y_sample, new_state_rwkv)
```

```python
import numpy as np
import ml_dtypes
import concourse.bass as bass
import concourse.mybir as mybir
from concourse.bass_utils import run_bass_kernel_spmd

F32, BF16 = mybir.dt.float32, mybir.dt.bfloat16
AF = mybir.ActivationFunctionType
ALU = mybir.AluOpType
AX = mybir.AxisListType
NPBF = ml_dtypes.bfloat16

NT = 4608
NCHUNK = 36
C0 = 0.6065306597126334
RMS_EPS = 1e-6
GN_EPS = 64e-5


class Res:
    __slots__ = ("w", "r")

    def __init__(self):
        self.w = None
        self.r = {}


class G:
    def __init__(self, nc):
        self.nc = nc
        self.E = {"pe": nc.tensor, "act": nc.scalar, "dve": nc.vector, "pool": nc.gpsimd, "sp": nc.sync}
        self.sem = {e: nc.alloc_semaphore("s_" + e) for e in ("pe", "act", "dve", "pool")}
        self.cnt = {e: 0 for e in self.sem}
        self.NDS = 12
        self.dsem = [nc.alloc_semaphore("d%d" % i) for i in range(2 * self.NDS)]
        self.dcnt = [0] * (2 * self.NDS)
        self.di = 0
        self.di_sw = 0
        self.seen = {e: {} for e in self.E}
        self.mute = False

    def _semof(self, k):
        return self.sem[k] if isinstance(k, str) else self.dsem[k]

    def _wait(self, e, toks):
        best = {}
        for k, v in toks:
            if k == e and e == "pe":
                continue
            if self.seen[e].get(k, 0) >= v:
                continue
            if best.get(k, 0) < v:
                best[k] = v
        for k, v in best.items():
            self.E[e].wait_ge(self._semof(k), v)
            self.seen[e][k] = v

    @staticmethod
    def _deps(reads, writes):
        toks = []
        for r in reads:
            if r.w:
                toks.append(r.w)
        for w in writes:
            if w.w:
                toks.append(w.w)
            toks.extend(w.r.items())
        return toks

    @staticmethod
    def _upd(tok, reads, writes):
        for r in reads:
            if r.r.get(tok[0], 0) < tok[1]:
                r.r[tok[0]] = tok[1]
        for w in writes:
            w.w = tok
            w.r = {}

    def op(self, e, fn, reads=(), writes=()):
        if self.mute:
            return
        self._wait(e, self._deps(reads, writes))
        ins = fn(self.E[e])
        self.cnt[e] += 1
        tok = (e, self.cnt[e])
        ins.then_inc(self.sem[e], 1)
        self._upd(tok, reads, writes)

    def dma(self, q, out, in_, reads=(), writes=()):
        if self.mute:
            return
        self._wait(q, self._deps(reads, writes))
        if q == "pool":
            i = self.NDS + self.di_sw
            self.di_sw = (self.di_sw + 1) % self.NDS
        else:
            i = self.di
            self.di = (self.di + 1) % self.NDS
        if self.dcnt[i]:
            self._wait(q, [(i, self.dcnt[i])])
        ins = self.E[q].dma_start(out=out, in_=in_)
        self.dcnt[i] += 16
        ins.then_inc(self.dsem[i], 16)
        self._upd((i, self.dcnt[i]), reads, writes)

    def barrier(self):
        toks = [(e, c) for e, c in self.cnt.items() if c] + [(i, c) for i, c in enumerate(self.dcnt) if c]
        for e in self.E:
            self._wait(e, [t for t in toks if t[0] != e])


def build(stop=99, debug=False, nhp=8, only3=False, stop3=99):
    nc = bass.Bass("TRN2", target_bir_lowering=False)
    g = G(nc)

    def din(name, shape, dt=F32):
        return nc.dram_tensor(name, list(shape), dt, kind="ExternalInput")

    x_all = din("x_all", [NT, 1024])
    cc = din("cc", [128, 8, 2])
    w_ada = din("w_ada", [1024, 3072])
    b_ada2 = din("b_ada2", [2, 3072])
    selw = din("selw", [2, 2, 128])
    ngb_d = din("ngb", [128, 1024])
    fgb_d = din("fgb", [128, 1024])
    w_in = din("w_in", [1024, 7296])
    ident_d = din("ident", [128, 128], BF16)
    bones_d = din("bones", [128, 128], BF16)
    bones2_d = din("bones2", [128, 2], BF16)
    mask4_d = din("mask4", [2, 128, 512], BF16)
    maskT_d = din("maskT", [2, 128, 128], BF16)
    bdm_d = din("bdm", [128, 128])
    I2_d = din("I2", [128, 512], BF16)
    mblk_d = din("mblk", [3, 2, 128, 256], BF16)
    rmask_d = din("rmask", [128, 1024])
    pcols_d = din("pcols", [128, 7, 8])
    mu6_d = din("mu6", [128, 25, 6])
    muf_d = din("muf", [128, 25])
    lnxg_d = din("lnxg", [128, 1024])
    lnxb_d = din("lnxb", [128, 1024])
    wup_d = din("wup", [64, 2, 1024])
    aup_d = din("aup", [64, 2, 1024])
    h0_d = din("h0bd", [2, 8, 128, 128])
    dft_d = din("dftL", [2, 8, 128, 32 * 512], BF16)
    dftc_d = din("dftC", [128, 2, 2, 256], BF16)
    c128_d = din("c128", [128, 2, 128], BF16)
    wpf_d = din("w_proj_f", [512, 1024])
    wpr_d = din("w_proj_r", [1024, 1024])
    wo_d = din("w_out", [1024, 1024])

    y_all = nc.dram_tensor("y_all", [NT, 1024], F32, kind="ExternalOutput")
    hst = nc.dram_tensor("hst", [2, 2, 8, 128, 128], F32, kind="ExternalOutput")

    sk = "ExternalOutput" if debug else "Internal"
    UT = nc.dram_tensor("UT", [57, 128, NT], BF16, kind="ExternalInput" if only3 else sk)
    XF = nc.dram_tensor("XF", [NT, 512], BF16, kind=sk)
    FZ = nc.dram_tensor("FZ", [4, 128, NT], BF16, kind=sk)
    ZT = nc.dram_tensor("ZT", [8, 128, NT], BF16, kind=sk)

    DBG = nc.dram_tensor("DBG", [128, 8 * NT], BF16, kind="ExternalOutput") if debug else None
    DBGF = nc.dram_tensor("DBGF", [128, 8192], F32, kind="ExternalOutput") if debug else None

    def dump(off, t_ap, res, n, f32=False):
        tgt = DBGF if f32 else DBG
        g.dma("sp", tgt[:, off:off + n], t_ap, (res,), ())

    def finish():
        for i in range(len(g.dcnt)):
            if g.dcnt[i]:
                nc.sync.wait_ge(g.dsem[i], g.dcnt[i])
        _CACHE['cnt'] = (dict(g.cnt), list(g.dcnt))
        return nc

    ABASE = 16384 + 128
    arena = {"off": ABASE, "n": 0}

    def sb(shape, dt, name=None):
        nbytes = int(np.prod(shape[1:])) * (4 if dt == F32 else 2)
        nbytes = (nbytes + 63) // 64 * 64
        arena["n"] += 1
        t = nc.alloc_sbuf_tensor_at("t%d_%s" % (arena["n"], name or ""), list(shape), dt, offset=arena["off"])
        arena["off"] += nbytes
        assert arena["off"] <= ABASE + 208 * 1024, arena["off"]
        return t, Res()

    ps = []
    for i in range(6):
        ps.append((nc.alloc_psum_tensor("ps%d" % i, [128, 512], F32), Res()))
    pt = []
    for i in range(2):
        pt.append((nc.alloc_psum_tensor("pt%d" % i, [128, 1024], BF16), Res()))

    rr = {"ps": 0, "pt": 0, "q": 0}

    rr["n"] = 4

    def nps():
        rr["ps"] = (rr["ps"] + 1) % rr["n"]
        return ps[rr["ps"]]

    def npt():
        rr["pt"] = (rr["pt"] + 1) % 2
        return pt[rr["pt"]]

    def mm(out, lhsT, rhs, start, stop, reads, writes):
        g.op("pe", lambda e: e.matmul(out, lhsT, rhs, start=start, stop=stop), reads, writes)

    def tr(out, in_, ident, reads, writes):
        g.op("pe", lambda e: e.transpose(out, in_, ident), reads, writes)

    def act(out, in_, func, reads, writes, bias=0.0, scale=1.0, accum=None):
        if accum is None:
            g.op("act", lambda e: e.activation(out, in_, func, bias=bias, scale=scale), reads, writes)
        else:
            g.op("act", lambda e: e.activation(out, in_, func, bias=bias, scale=scale, accum_out=accum), reads, writes)

    def tt(eng, out, a, b, op, reads, writes):
        g.op(eng, lambda e: e.tensor_tensor(out, a, b, op), reads, writes)

    def ts(eng, out, a, s1, s2, op0, op1, reads, writes):
        if s2 is None:
            s2, op1 = 0.0, ALU.add
        g.op(eng, lambda e: e.tensor_scalar(out, a, s1, s2, op0, op1), reads, writes)

    def stt(eng, out, a, s, b, op0, op1, reads, writes):
        eng = "dve"
        g.op(eng, lambda e: e.scalar_tensor_tensor(out, a, s, b, op0, op1), reads, writes)

    def cp(eng, out, in_, reads, writes):
        if eng == "act":
            g.op("act", lambda e: e.copy(out, in_), reads, writes)
        else:
            g.op(eng, lambda e: e.tensor_copy(out, in_), reads, writes)

    def ld(out, in_, res, q="sp"):
        g.dma(q, out, in_, (), (res,))

    ident, r_ident = sb([128, 128], BF16, "ident")
    bones, r_bones = sb([128, 128], BF16, "bones")
    bones2, r_bones2 = sb([128, 2], BF16, "bones2")
    mask4, r_mask4 = sb([128, 2, 512], BF16, "mask4")
    maskT, r_maskT = sb([128, 2, 128], BF16, "maskT")
    bdm, r_bdm = sb([128, 128], F32, "bdm")
    pcols, r_pcols = sb([128, 7, 8], F32, "pcols")
    oka, r_oka = sb([128, 8], F32, "oka")
    Gbc, r_Gbc = sb([128, 2, 1024], F32, "Gbc")
    fgb, r_fgb = sb([128, 1024], F32, "fgb")
    ld(ident[:], ident_d[:, :], r_ident)
    ld(bones[:], bones_d[:, :], r_bones)
    ld(bones2[:], bones2_d[:, :], r_bones2)
    for d in range(2):
        ld(mask4[:, d, :], mask4_d[d, :, :], r_mask4)
        ld(maskT[:, d, :], maskT_d[d, :, :], r_maskT)
    ld(bdm[:], bdm_d[:, :], r_bdm)
    ld(pcols[:], pcols_d[:, :, :], r_pcols)
    ld(fgb[:], fgb_d[:, :], r_fgb)
    ts("dve", oka[:], pcols[:, 1, :], -1.0, 1.0, ALU.mult, ALU.add, (r_pcols,), (r_oka,))
    PERSIST = arena["off"]

    g.mute = only3
    hT, r_hT = sb([128, 8, NT], BF16, "hT")
    P1BASE = arena["off"]
    ccs, r_ccs = sb([128, 8, 2], F32, "ccs")
    ld(ccs[:], cc[:, :, :], r_ccs)
    act(ccs[:], ccs[:], AF.Silu, (r_ccs,), (r_ccs,))
    wad = [sb([128, 8, 512], F32, "wad%d" % i) for i in range(2)]
    mrows, r_mrows = sb([2, 3072], F32, "mrows")
    bad, r_bad = sb([2, 3072], F32, "bad")
    selw_s, r_selw = sb([2, 2, 128], F32, "selw")
    ngb, r_ngb = sb([128, 1024], F32, "ngb")
    Abc, r_Abc = sb([128, 2, 1024], F32, "Abc")
    Bbc, r_Bbc = sb([128, 2, 1024], F32, "Bbc")
    ld(bad[:], b_ada2[:, :], r_bad)
    ld(selw_s[:], selw[:, :, :], r_selw)
    ld(ngb[:], ngb_d[:, :], r_ngb)
    w_ada_v = w_ada.ap().rearrange("(kc p) n -> p kc n", p=128)
    for n in range(6):
        wt, rw = wad[n % 2]
        ld(wt[:], w_ada_v[:, :, n * 512:(n + 1) * 512], rw, q="sp" if n % 2 == 0 else "pool")
        p_, rp = nps()
        for kc in range(8):
            mm(p_[0:2, :], ccs[:, kc, :], wt[:, kc, :], kc == 0, kc == 7, (r_ccs, rw), (rp,))
        tt("dve", mrows[:, n * 512:(n + 1) * 512], p_[0:2, :], bad[:, n * 512:(n + 1) * 512], ALU.add,
           (rp, r_bad), (r_mrows,))
    for which in range(2):
        for kind in range(3):
            for half in range(2):
                p_, rp = nps()
                c0 = kind * 1024 + half * 512
                mm(p_[:, :], selw_s[:, which, :], mrows[:, c0:c0 + 512], True, True, (r_selw, r_mrows), (rp,))
                hs = slice(half * 512, half * 512 + 512)
                if kind == 0:
                    cp("act", Bbc[:, which, hs], p_[:, :], (rp,), (r_Bbc,))
                elif kind == 1:
                    stt("dve", Abc[:, which, hs], p_[:, :], 1.0, ngb[:, hs], ALU.add, ALU.mult, (rp, r_ngb), (r_Abc,))
                else:
                    cp("act", Gbc[:, which, hs], p_[:, :], (rp,), (r_Gbc,))

    xt = [sb([128, 1024], F32, "xt%d" % i) for i in range(2)]
    junk, r_junk = sb([128, 1024], BF16, "junk")
    t1, r_t1 = sb([128, 1024], F32, "t1")
    hb = [sb([128, 1024], BF16, "hb%d" % i) for i in range(2)]
    ssq = [sb([128, 1], F32, "ss%d" % i) for i in range(2)]
    rst = [sb([128, 1], F32, "rst%d" % i) for i in range(2)]
    for i in range(NCHUNK):
        which = 0 if i < 32 else 1
        x_, rx = xt[i % 2]
        ld(x_[:], x_all[i * 128:(i + 1) * 128, :], rx, q="sp" if i % 2 == 0 else "pool")
        s_, rs_ = ssq[i % 2]
        q_, rq = rst[i % 2]
        act(junk[:], x_[:], AF.Square, (rx,), (r_junk, rs_), accum=s_[:])
        act(q_[:], s_[:], AF.Sqrt, (rs_,), (rq,), bias=RMS_EPS, scale=1.0 / 1024)
        g.op("dve", lambda e: e.reciprocal(q_[:], q_[:]), (rq,), (rq,))
        stt("dve", t1[:], x_[:], q_[:, 0:1], Abc[:, which, :], ALU.mult, ALU.mult, (rx, rq, r_Abc), (r_t1,))
        h_, rh = hb[i % 2]
        tt("pool", h_[:], t1[:], Bbc[:, which, :], ALU.add, (r_t1, r_Bbc), (rh,))
        p_, rp = npt()
        for kc in range(8):
            tr(p_[:, kc * 128:(kc + 1) * 128], h_[:, kc * 128:(kc + 1) * 128], ident[:], (rh, r_ident), (rp,))
        cp("act" if i % 2 else "dve", hT[:, :, i * 128:(i + 1) * 128],
           p_[:, :].rearrange("p (k t) -> p k t", k=8), (rp,), (r_hT,))

    if stop == 1:
        if debug:
            g.dma("sp", DBG[:, :], hT[:].rearrange("p k t -> p (k t)"), (r_hT,), ())
        return finish()
    g.barrier()
    arena["off"] = P1BASE
    w_in_v = w_in.ap().rearrange("(kc p) n -> p kc n", p=128)
    wxf32, r_wxf32 = sb([128, 8, 512], F32, "wxf32")
    wxf, r_wxf = sb([128, 8, 512], BF16, "wxf")
    ld(wxf32[:], w_in_v[:, :, 0:512], r_wxf32)
    cp("dve", wxf[:], wxf32[:], (r_wxf32,), (r_wxf,))
    xfst = [sb([128, 512], BF16, "xfst%d" % i) for i in range(2)]
    for i in range(NCHUNK):
        p_, rp = nps()
        for kc in range(8):
            mm(p_[:, :], hT[:, kc, i * 128:(i + 1) * 128], wxf[:, kc, :], kc == 0, kc == 7, (r_hT, r_wxf), (rp,))
        s_, rs_ = xfst[i % 2]
        cp("act" if i % 2 else "dve", s_[:], p_[:, :], (rp,), (rs_,))
        g.dma("sp", XF[i * 128:(i + 1) * 128, :], s_[:], (rs_,), ())
    w32 = [sb([128, 8, 128], F32, "w32_%d" % i) for i in range(2)]
    wb = [sb([128, 8, 128], BF16, "wb%d" % i) for i in range(2)]
    stage = [sb([128, NT], BF16, "stage%d" % i) for i in range(2)]
    for ct in range(4, 57):
        k = ct % 2
        w_, rw = w32[k]
        ld(w_[:], w_in_v[:, :, ct * 128:(ct + 1) * 128], rw, q="pool" if k else "sp")
        b_, rb = wb[k]
        cp("pool" if k else "dve", b_[:], w_[:], (rw,), (rb,))
        if ct < 8 or 33 <= ct < 41:
            fn = AF.Silu
        elif ct >= 41:
            fn = AF.Sigmoid
        else:
            fn = AF.Copy
        st_, rs_ = stage[k]
        for blk in range(9):
            p_, rp = nps()
            cs_ = slice(blk * 512, blk * 512 + 512)
            for kc in range(8):
                mm(p_[:, :], b_[:, kc, :], hT[:, kc, cs_], kc == 0, kc == 7, (rb, r_hT), (rp,))
            if fn == AF.Copy:
                cp("dve" if blk % 2 == 0 else "act", st_[:, cs_], p_[:, :], (rp,), (rs_,))
            else:
                act(st_[:, cs_], p_[:, :], fn, (rp,), (rs_,))
        g.dma("sp", UT[ct, :, :], st_[:], (rs_,), ())

    if stop == 2:
        return finish()
    g.barrier()
    arena["off"] = PERSIST
    xftok, r_xftok = sb([128, NCHUNK, 512], BF16, "xftok")
    gfT, r_gfT = sb([128, 4, NT], BF16, "gfT")
    ld(xftok[:], XF.ap().rearrange("(i p) c -> p i c", p=128), r_xftok)
    for gi in range(4):
        ld(gfT[:, gi, :], UT[4 + gi, :, :], r_gfT, q="pool")
    tabs = [sb([128, 32, 512], BF16, "tab%d" % i) for i in range(2)]
    ctab, r_ctab = sb([128, 2, 2, 256], BF16, "ctab")
    c128, r_c128 = sb([128, 2, 128], BF16, "c128")
    ld(ctab[:], dftc_d[:, :, :, :], r_ctab)
    ld(c128[:], c128_d[:, :, :], r_c128)
    Pb = [[sb([128, 512], BF16, "P%d_%d" % (a, b)) for b in range(4)] for a in range(2)]
    fzst = [sb([128, 512], BF16, "fzst%d" % i) for i in range(2)]
    nld = 0

    def fourier_tail(ncol, col0):
        for gi in range(4):
            p_, rp = ps[4 + gi % 2]
            mm(p_[:, 0:ncol], c128[:, 0, :], Pb[0][gi][0][:, 0:ncol], True, False, (r_c128, Pb[0][gi][1]), (rp,))
            mm(p_[:, 0:ncol], c128[:, 1, :], Pb[1][gi][0][:, 0:ncol], False, True, (r_c128, Pb[1][gi][1]), (rp,))
            s_, rs_ = fzst[gi % 2]
            tt("dve", s_[:, 0:ncol], p_[:, 0:ncol], gfT[:, gi, col0:col0 + ncol], ALU.mult, (rp, r_gfT), (rs_,))
            g.dma("sp", FZ[gi, :, col0:col0 + ncol], s_[:, 0:ncol], (rs_,), ())

    for pb in range(8):
        for cs_i in range(2):
            tb_, rt_ = tabs[nld % 2]
            ld(tb_[:], dft_d[cs_i, pb, :, :].rearrange("p (t f) -> p t f", t=32), rt_, q="pool" if nld % 2 else "sp")
            nld += 1
            for tk in range(32):
                for gi in range(4):
                    mm(ps[gi][0][:, :], xftok[:, tk, gi * 128:(gi + 1) * 128], tb_[:, tk, :], tk == 0, tk == 31,
                       (r_xftok, rt_), (ps[gi][1],))
            for gi in range(4):
                cp("act" if gi % 2 else "dve", Pb[cs_i][gi][0][:], ps[gi][0][:, :], (ps[gi][1],), (Pb[cs_i][gi][1],))
        fourier_tail(512, pb * 512)
    for sq in range(2):
        for cs_i in range(2):
            for gi in range(4):
                for tk in range(2):
                    mm(ps[gi][0][:, 0:256], xftok[:, 32 + 2 * sq + tk, gi * 128:(gi + 1) * 128], ctab[:, cs_i, tk, :],
                       tk == 0, tk == 1, (r_xftok, r_ctab), (ps[gi][1],))
                cp("act" if gi % 2 else "dve", Pb[cs_i][gi][0][:, 0:256], ps[gi][0][:, 0:256], (ps[gi][1],),
                   (Pb[cs_i][gi][1],))
        fourier_tail(256, 4096 + 256 * sq)

    if stop == 3:
        return finish()
    g.mute = False
    g.barrier()
    arena["off"] = PERSIST
    rmask, r_rmask = sb([128, 512], F32, "rmask")
    mu6, r_mu6 = sb([128, 25, 6], F32, "mu6")
    muf, r_muf = sb([128, 25], F32, "muf")
    om, r_om = sb([128, 25], F32, "om")
    lnxg, r_lnxg = sb([128, 128], F32, "lnxg")
    lnxb, r_lnxb = sb([128, 128], F32, "lnxb")
    wup32, r_wup32 = sb([128, 2, 1024], F32, "wup32")
    wupb, r_wupb = sb([128, 2, 1024], BF16, "wupb")
    ld(rmask[:], rmask_d[:, 0:512], r_rmask)
    ld(mu6[:], mu6_d[:, :, :], r_mu6)
    ld(muf[:], muf_d[:, :], r_muf)
    ld(wup32[0:64, :, :], wup_d[:, :, :], r_wup32)
    ld(wup32[64:128, :, :], aup_d[:, :, :], r_wup32)
    cp("dve", wupb[:], wup32[:], (r_wup32,), (r_wupb,))
    ts("dve", om[:], muf[:], -1.0, 1.0, ALU.mult, ALU.add, (r_muf,), (r_om,))

    raw, r_raw = sb([128, NT], BF16, "raw")
    rsh, r_rsh = sb([128, NT], BF16, "rsh")
    ksh, r_ksh = sb([128, NT], BF16, "ksh")
    vsh, r_vsh = sb([128, NT], BF16, "vsh")
    wdad, r_wdad = sb([128, NT], BF16, "wdad")
    kk, r_kk = sb([128, NT], BF16, "kk")
    rkb, r_rkb = raw, r_raw
    grT, r_grT = sb([128, NT], BF16, "grT")
    zrow, r_zrow = sb([128, NT], BF16, "zrow")
    Vtok, r_Vtok = sb([128, NCHUNK, 128], BF16, "Vtok")
    sbon, r_sbon = sb([128, NCHUNK, 2], F32, "sbon")
    yf, r_yf = sb([128, NCHUNK, 128], F32, "yf")
    t_asig, r_asig = sb([128, 512], F32, "t_asig")
    t_sg, r_sg = sb([128, 512], F32, "t_sg")
    t_cs, r_cs = sb([128, 512], F32, "t_cs")
    t_p, r_p = sb([128, 512], F32, "t_p")
    t_e, r_e = sb([128, 512], F32, "t_e")
    ARs = [sb([128, 4, 2, 128], BF16, "AR%d" % i_) for i_ in range(2)]
    AR, r_AR = ARs[0]
    bt, r_bt = sb([128, 512], BF16, "bt")
    kt, r_kt = sb([128, 512], BF16, "kt")
    Btoks = [sb([128, 4, 128], BF16, "Btok%d" % i_) for i_ in range(2)]
    Btok, r_Btok = Btoks[0]
    Ktoks = [sb([128, 4, 128], BF16, "Ktok%d" % i_) for i_ in range(2)]
    Ktok, r_Ktok = Ktoks[0]
    gams = [sb([128, 8], F32, "gam%d" % i_) for i_ in range(2)]
    gam, r_gam = gams[0]
    NU = 8
    ATu = [sb([128, 512], BF16, "AT%d" % u) for u in range(NU)]
    NP_ = 4
    WSB = [sb([128, 256], BF16, "wsb%d" % i) for i in range(2)]
    ASB = [sb([128, 512], BF16, "asb%d" % i) for i in range(2)]
    XNp = [[sb([128, 512], BF16, "XN%d_%d" % (u, i)) for i in range(2)] for u in range(NP_)]
    PQp = [[sb([128, 512], BF16, "PQ%d_%d" % (u, i)) for i in range(2)] for u in range(NP_)]
    DNp = [sb([128, 512], BF16, "DN%d" % u) for u in range(NP_)]
    O1p = [sb([128, 512], BF16, "O1_%d" % u) for u in range(NP_)]
    O2p = [sb([128, 512], BF16, "O2_%d" % u) for u in range(NP_)]
    YZp = [sb([128, 512], BF16, "YZ%d" % u) for u in range(NP_)]
    Ttp = [sb([128, 2, 128], BF16, "Tt%d" % u) for u in range(NP_)]
    MD, r_MD = sb([128, 2, 256], BF16, "MD")
    MO1, r_MO1 = sb([128, 2, 256], BF16, "MO1")
    MO2, r_MO2 = sb([128, 2, 256], BF16, "MO2")
    I2, r_I2 = sb([128, 512], BF16, "I2")
    ld(I2[:], I2_d[:, :], r_I2)
    for d_ in range(2):
        ld(MD[:, d_, :], mblk_d[0, d_, :, :], r_MD)
        ld(MO1[:, d_, :], mblk_d[1, d_, :, :], r_MO1)
        ld(MO2[:, d_, :], mblk_d[2, d_, :, :], r_MO2)
    ucnt = [0]
    Hf, r_Hf = sb([128, 128], F32, "Hf")
    H1, r_H1 = sb([128, 128], F32, "H1")
    Hm, r_Hm = sb([128, 128], F32, "Hm")
    Hb, r_Hb = sb([128, 128], BF16, "Hb")
    X0b, r_X0b = sb([128, 128], BF16, "X0b")
    Ub, r_Ub = sb([128, 128], BF16, "Ub")
    YS = [sb([128, 128], F32, "ysum%d" % i_) for i_ in range(2)]
    YQ = [sb([128, 128], F32, "ysq%d" % i_) for i_ in range(2)]
    YN = [sb([128, 128], F32, "yn%d" % i_) for i_ in range(2)]
    ZP = [sb([128, 128], BF16, "zpre%d" % i_) for i_ in range(2)]
    ST4 = [sb([128, 8], F32, "st4_%d" % i_) for i_ in range(2)]
    tailcnt = [0]
    pending = []
    tmpa, r_tmpa = t_asig, r_asig
    tmpb, r_tmpb = bt, r_bt
    tmpc, r_tmpc = t_sg, r_sg
    rr["n"] = 4
    rS = [Res() for _ in range(4)]

    def shift_rows(eng, dst, rdst, tile):
        ts(eng, dst[:], raw[:], om[:, tile:tile + 1], None, ALU.mult, ALU.bypass, (r_raw, r_om), (rdst,))
        d3 = dst[:, 0:4096].rearrange("p (r w) -> p r w", w=64)
        s3 = raw[:, 0:4096].rearrange("p (r w) -> p r w", w=64)
        pairs = [(d3[:, :, 1:64], s3[:, :, 0:63], 0), (d3[:, :, 0:63], s3[:, :, 1:64], 1),
                 (d3[:, 1:64, :], s3[:, 0:63, :], 2), (d3[:, 0:63, :], s3[:, 1:64, :], 3)]
        dc = dst[:, 4096:NT].rearrange("p (s t) -> p s t", t=256)
        sc = raw[:, 4096:NT].rearrange("p (s t) -> p s t", t=256)
        pairs += [(dc[:, :, 1:256], sc[:, :, 0:255], 4), (dc[:, :, 0:255], sc[:, :, 1:256], 5)]
        for o_, i_, m in pairs:
            stt(eng, o_, i_, mu6[:, tile, m:m + 1], o_, ALU.mult, ALU.add, (r_raw, r_mu6, rdst), (rdst,))

    ld(raw[:], UT[32, :, :], r_raw)
    shift_rows("dve", wdad, r_wdad, 24)
    act(wdad[0:64, :], wdad[0:64, :], AF.Tanh, (r_wdad,), (r_wdad,))
    if stop3 == 1:
        dump(0, wdad[:], r_wdad, NT)
        return finish()

    SEGS = [(4 * i_, 4) for i_ in range(9)]

    for hp in range(nhp):
        hc = slice(hp * 128, hp * 128 + 128)
        for (tile, dst, rdst, eng) in ((hp, rsh, r_rsh, "dve"), (8 + hp, ksh, r_ksh, "pool"), (16 + hp, vsh, r_vsh, "dve")):
            ld(raw[:], UT[8 + tile, :, :], r_raw)
            shift_rows(eng, dst, rdst, tile)
        ld(grT[:], UT[33 + hp, :, :], r_grT, q="pool")
        ld(lnxg[:], lnxg_d[:, hc], r_lnxg, q="pool")
        ld(lnxb[:], lnxb_d[:, hc], r_lnxb, q="pool")
        for blk in range(9):
            cs_ = slice(blk * 512, blk * 512 + 512)
            ts("dve", tmpa[:], ksh[:, cs_], pcols[:, 0, hp:hp + 1], None, ALU.mult, ALU.bypass, (r_ksh, r_pcols), (r_tmpa,))
            act(tmpb[:], tmpa[:], AF.Square, (r_tmpa,), (r_tmpb,))
            p_, rp = nps()
            mm(p_[:, :], bones[:], tmpb[:], True, True, (r_bones, r_tmpb), (rp,))
            act(tmpc[:], p_[:, :], AF.Sqrt, (rp,), (r_tmpc,))
            ts("dve", tmpc[:], tmpc[:], 1e-12, None, ALU.max, ALU.bypass, (r_tmpc,), (r_tmpc,))
            g.op("dve", lambda e: e.reciprocal(tmpc[:], tmpc[:]), (r_tmpc,), (r_tmpc,))
            tt("dve", kk[:, cs_], tmpa[:], tmpc[:], ALU.mult, (r_tmpa, r_tmpc), (r_kk,))
        stt("pool", rkb[:], rsh[:], pcols[:, 2, hp:hp + 1], ksh[:], ALU.mult, ALU.mult, (r_rsh, r_pcols, r_ksh), (r_rkb,))
        for c in range(NCHUNK):
            cc_ = slice(c * 128, c * 128 + 128)
            p_, rp = npt()
            tr(p_[:, 0:128], vsh[:, cc_], ident[:], (r_vsh, r_ident), (rp,))
            cp("act", Vtok[:, c, :], p_[:, 0:128], (rp,), (r_Vtok,))
            q_, rq = nps()
            mm(q_[:, 0:2], rkb[:, cc_], bones2[:], True, True, (r_rkb, r_bones2), (rq,))
            cp("dve", sbon[:, c, :], q_[:, 0:2], (rq,), (r_sbon,))

        if stop3 == 2:
            for i_, (t_, r_) in enumerate(((rsh, r_rsh), (ksh, r_ksh), (vsh, r_vsh), (kk, r_kk))):
                dump(i_ * NT, t_[:], r_, NT)
            dump(4 * NT, Vtok[:].rearrange("p c t -> p (c t)"), r_Vtok, NT)
            dump(0, sbon[:].rearrange("p c t -> p (c t)"), r_sbon, 72, f32=True)
            return finish()
        for d in range(2):
            segs = SEGS[0:8] if d == 0 else SEGS[7::-1]
            segs = list(segs) + [SEGS[8]]
            def derived(ch0, nch, par):
                AR, r_AR = ARs[par]
                Btok, r_Btok = Btoks[par]
                Ktok, r_Ktok = Ktoks[par]
                gam, r_gam = gams[par]
                col0 = ch0 * 128
                ncol = nch * 128
                sc_ = slice(col0, col0 + ncol)
                for b0 in range(0, ncol, 512):
                    bs = slice(col0 + b0, col0 + b0 + 512)
                    ls = slice(b0, b0 + 512)
                    p_, rp = nps()
                    mm(p_[:, :], wupb[64:128, d, hc], wdad[64:128, bs], True, True, (r_wupb, r_wdad), (rp,))
                    act(t_asig[:, ls], p_[:, :], AF.Sigmoid, (rp, r_pcols), (r_asig,), bias=pcols[:, 5 + d, hp:hp + 1])
                    p_, rp = nps()
                    mm(p_[:, :], wupb[0:64, d, hc], wdad[0:64, bs], True, True, (r_wupb, r_wdad), (rp,))
                    act(t_sg[:, ls], p_[:, :], AF.Sigmoid, (rp, r_pcols), (r_sg,), bias=pcols[:, 3 + d, hp:hp + 1])
                L = slice(0, ncol)
                yield
                g.op("dve", lambda e: e.tensor_tensor_scan(t_cs[:, L], rmask[:, L], t_sg[:, L], 0.0, ALU.mult, ALU.add),
                     (r_rmask, r_sg), (r_cs,))
                yield
                cs3 = t_cs[:, L].rearrange("p (c t) -> p c t", t=128)
                act(gam[:, 0:nch], cs3[:, :, 127], AF.Exp, (r_cs,), (r_gam,), scale=-C0)
                yield
                v3 = lambda a: a.rearrange("p (c t) -> p c t", t=128)
                if d == 0:
                    tt("pool", t_p[:, L], t_cs[:, L], t_sg[:, L], ALU.subtract, (r_cs, r_sg), (r_p,))
                    pin, rpin = t_cs, r_cs
                else:
                    for ci in range(nch):
                        l1 = slice(ci * 128, ci * 128 + 128)
                        ts("pool", t_p[:, l1], t_cs[:, l1], -1.0, t_cs[:, ci * 128 + 127:ci * 128 + 128], ALU.mult, ALU.add,
                           (r_cs,), (r_p,))
                pex, rpex = t_p, r_p
                act(t_e[:, L], pex[:, L], AF.Exp, (rpex,), (r_e,), scale=-C0)
                stt("dve", AR[:, 0:nch, 0, :], v3(kk[:, sc_]), -1.0, v3(t_e[:, L]), ALU.mult, ALU.mult, (r_kk, r_e), (r_AR,))
                yield
                if d == 1:
                    tt("pool", t_p[:, L], t_p[:, L], t_sg[:, L], ALU.add, (r_p, r_sg), (r_p,))
                    pin, rpin = t_p, r_p
                act(t_e[:, L], pin[:, L], AF.Exp, (rpin,), (r_e,), scale=-C0)
                yield
                tt("dve", AR[:, 0:nch, 1, :], v3(rsh[:, sc_]), v3(t_e[:, L]), ALU.mult, (r_rsh, r_e), (r_AR,))
                yield
                act(t_e[:, L], pin[:, L], AF.Exp, (rpin,), (r_e,), scale=C0)
                tt("dve", t_sg[:, L], kk[:, sc_], t_asig[:, L], ALU.mult, (r_kk, r_asig), (r_sg,))
                tt("dve", bt[:, L], t_sg[:, L], t_e[:, L], ALU.mult, (r_sg, r_e), (r_bt,))
                yield
                ts("pool", t_asig[:, L], t_asig[:, L], pcols[:, 1, hp:hp + 1], oka[:, hp:hp + 1], ALU.mult, ALU.add,
                   (r_asig, r_pcols, r_oka), (r_asig,))
                tt("pool", t_asig[:, L], t_asig[:, L], ksh[:, sc_], ALU.mult, (r_asig, r_ksh), (r_asig,))
                yield
                tt("dve", kt[:, L], t_asig[:, L], t_e[:, L], ALU.mult, (r_asig, r_e), (r_kt,))
                yield "T"
                for ci in range(nch):
                    l1 = slice(ci * 128, ci * 128 + 128)
                    p_, rp = npt()
                    tr(p_[:, 0:128], bt[:, l1], ident[:], (r_bt, r_ident), (rp,))
                    cp("act", Btok[:, ci, :], p_[:, 0:128], (rp,), (r_Btok,))
                    p_, rp = npt()
                    tr(p_[:, 0:128], kt[:, l1], ident[:], (r_kt, r_ident), (rp,))
                    cp("dve", Ktok[:, ci, :], p_[:, 0:128], (rp,), (r_Ktok,))

            gen0 = derived(segs[0][0], segs[0][1], 0)
            for _ in gen0:
                pass
            for si, (ch0, nch) in enumerate(segs):
                par = si % 2
                AR, r_AR = ARs[par]
                Btok, r_Btok = Btoks[par]
                Ktok, r_Ktok = Ktoks[par]
                gam, r_gam = gams[par]
                nxt = derived(segs[si + 1][0], segs[si + 1][1], 1 - par) if si + 1 < len(segs) else iter(())
                atT = [False]

                def adv(nxt=nxt, atT=atT):
                    if not atT[0]:
                        if next(nxt, None) == "T":
                            atT[0] = True
                v2 = lambda ap_: ap_.rearrange("p (h c) -> p h c", h=2)
                for ci in range(nch):
                    l1 = slice(ci * 128, ci * 128 + 128)
                    dn, rdn = DNp[ci]
                    for h in range(2):
                        u = ci * 2 + h
                        hs = slice(64 * h, 64 * h + 64)
                        AT, rAT = ATu[u]
                        p_, rp = nps()
                        arv = AR[hs, ci, :, :].rearrange("p a t -> p (a t)")
                        mm(p_[:, 0:256], bt[hs, l1], arv, True, True, (r_bt, r_AR), (rp,))
                        mm(p_[:, 256:512], kt[hs, l1], arv, True, True, (r_kt, r_AR), (rp,))
                        asb, rasb = ASB[u % 2]
                        cp("act", asb[:], p_[:, :], (rp,), (rasb,))
                        tt("dve", AT[:], asb[:], mask4[:, d, :], ALU.mult, (rasb, r_mask4), (rAT,))
                        w_, rw = nps()
                        mm(w_[:, 0:128], AR[hs, ci, 0, :], bt[hs, l1], True, True, (r_AR, r_bt), (rw,))
                        mm(w_[:, 128:256], bt[hs, l1], AR[hs, ci, 0, :], True, True, (r_AR, r_bt), (rw,))
                        o = 256 * h
                        wsb, rwsb = WSB[u % 2]
                        cp("act", wsb[:], w_[:, 0:256], (rw,), (rwsb,))
                        tt("dve", dn[:, o:o + 256], wsb[:], MD[:, d, 0:256], ALU.mult, (rwsb, r_MD), (rdn,))
                        tt("dve", O1p[ci][0][:, o:o + 256], wsb[:], MO1[:, d, 0:256], ALU.mult, (rwsb, r_MO1), (O1p[ci][1],))
                        tt("dve", O2p[ci][0][:, o:o + 256], wsb[:], MO2[:, d, 0:256], ALU.mult, (rwsb, r_MO2), (O2p[ci][1],))
                    pq, rpq = PQp[ci][0]
                    tt("pool", pq[:], dn[:], I2[:], ALU.add, (rdn, r_I2), (rpq,))
                for lev in range(4):
                    for ci in range(nch):
                        xn, rxn = DNp[ci] if lev == 0 else XNp[ci][lev % 2]
                        xn2, rxn2 = XNp[ci][(lev + 1) % 2]
                        p2, rp2 = nps()
                        for h in range(2):
                            o = 256 * h
                            mm(p2[:, o:o + 128], xn[:, o + 128:o + 256], xn[:, o:o + 128], True, True, (rxn,), (rp2,))
                            mm(p2[:, o + 128:o + 256], xn[:, o:o + 128], xn[:, o + 128:o + 256], True, True, (rxn,), (rp2,))
                        cp("act", xn2[:], p2[:, :], (rp2,), (rxn2,))
                        adv()
                    for ci in range(nch):
                        xn2, rxn2 = XNp[ci][(lev + 1) % 2]
                        pq, rpq = PQp[ci][lev % 2]
                        pq2, rpq2 = PQp[ci][(lev + 1) % 2]
                        q_, rq = nps()
                        for h in range(2):
                            o = 256 * h
                            mm(q_[:, o:o + 128], xn2[:, o + 128:o + 256], pq[:, o:o + 128], True, True, (rxn2, rpq), (rq,))
                            mm(q_[:, o + 128:o + 256], xn2[:, o:o + 128], pq[:, o + 128:o + 256], True, True, (rxn2, rpq), (rq,))
                        qsb, rqsb = ASB[ci % 2]
                        cp("act", qsb[:], q_[:, :], (rq,), (rqsb,))
                        tt("dve", pq2[:], qsb[:], pq[:], ALU.add, (rqsb, rpq), (rpq2,))
                        adv()
                for ci in range(nch):
                    pq, rpq = PQp[ci][0]
                    o1, ro1 = O1p[ci]
                    yz, ryz = YZp[ci]
                    p2, rp2 = nps()
                    for h in range(2):
                        o = 256 * h
                        mm(p2[:, o:o + 128], o1[:, o + 128:o + 256], pq[:, o:o + 128], True, True, (ro1, rpq), (rp2,))
                        mm(p2[:, o + 128:o + 256], o1[:, o:o + 128], pq[:, o + 128:o + 256], True, True, (ro1, rpq), (rp2,))
                    cp("act", yz[:], p2[:, :], (rp2,), (ryz,))
                for ci in range(nch):
                    pq, rpq = PQp[ci][0]
                    pq2, rpq2 = PQp[ci][1]
                    yz, ryz = YZp[ci]
                    q_, rq = nps()
                    for h in range(2):
                        o = 256 * h
                        mm(q_[:, o:o + 128], pq[:, o + 128:o + 256], yz[:, o:o + 128], True, True, (rpq, ryz), (rq,))
                        mm(q_[:, o + 128:o + 256], pq[:, o:o + 128], yz[:, o + 128:o + 256], True, True, (rpq, ryz), (rq,))
                    tt("dve", pq2[:], q_[:, :], pq[:], ALU.add, (rq, rpq), (rpq2,))
                for ci in range(nch):
                    pq, rpq = PQp[ci][1]
                    o2, ro2 = O2p[ci]
                    yz, ryz = YZp[ci]
                    p2, rp2 = nps()
                    for h in range(2):
                        o = 256 * h
                        mm(p2[:, o:o + 128], o2[:, o:o + 128], pq[:, o + 128:o + 256], True, True, (ro2, rpq), (rp2,))
                    cp("act", v2(yz[:])[:, :, 0:128], v2(p2[:, :])[:, :, 0:128], (rp2,), (ryz,))
                for ci in range(nch):
                    pq, rpq = PQp[ci][1]
                    yz, ryz = YZp[ci]
                    q_, rq = nps()
                    for h in range(2):
                        o = 256 * h
                        mm(q_[:, o:o + 128], pq[:, o:o + 128], yz[:, o:o + 128], True, True, (rpq, ryz), (rq,))
                    tt("dve", Ttp[ci][0][:], v2(q_[:, :])[:, :, 0:128], v2(pq[:])[:, :, 128:256], ALU.add, (rq, rpq), (Ttp[ci][1],))
                for _ in nxt:
                    pass
                if ch0 < 32:
                    order = list(range(nch)) if d == 0 else list(range(nch - 1, -1, -1))
                    seqs = [(-1, order)]
                else:
                    seqs = [(0, [0, 1] if d == 0 else [1, 0]), (1, [2, 3] if d == 0 else [3, 2])]
                for (sq, order) in seqs:
                    for oi, ci in enumerate(order):
                        gc = ch0 + ci
                        l1 = slice(ci * 128, ci * 128 + 128)
                        first_lat = (sq == -1 and oi == 0 and ch0 == (0 if d == 0 else 28))
                        if first_lat:
                            ld(Hf[:], h0_d[d, hp, :, :], r_Hf)
                            cp("act", Hb[:], Hf[:], (r_Hf,), (r_Hb,))
                        elif sq >= 0 and oi == 0:
                            g.op("dve", lambda e: e.memset(Hf[:], 0.0), (), (r_Hf,))
                            g.op("pool", lambda e: e.memset(Hb[:], 0.0), (), (r_Hb,))
                        Tt = [(Ttp[ci][0][:, 0, :], Ttp[ci][1]), (Ttp[ci][0][:, 1, :], Ttp[ci][1])]
                        ATb = [ATu[ci * 2], ATu[ci * 2 + 1]]
                        ucnt[0] += 1
                        psXU, psY = ps[4][0], ps[5][0]
                        X0p, Up, Yp, Pp = (psXU[:, 0:128], psXU[:, 128:256], psY[:, 0:128], psY[:, 256:384])
                        mm(X0p, AR[:, ci, 0, :], Hb[:], True, False, (r_AR, r_Hb), (rS[0],))
                        for h in range(2):
                            hcs = slice(64 * h, 64 * h + 64)
                            mm(psXU[:, hcs], ATb[h][0][:, 256:384], Vtok[:, gc, hcs], False, h == 1, (ATb[h][1], r_Vtok), (rS[0],))
                        cp("act", X0b[:], X0p, (rS[0],), (r_X0b,))
                        for h in range(2):
                            mm(psXU[:, 128 + 64 * h:192 + 64 * h], Tt[h][0], X0b[:, 64 * h:64 * h + 64], True, True,
                               (Tt[h][1], r_X0b), (rS[1],))
                        cp("act", Ub[:], Up, (rS[1],), (r_Ub,))
                        mm(Yp, AR[:, ci, 1, :], Hb[:], True, False, (r_AR, r_Hb), (rS[2],))
                        for h in range(2):
                            hcs = slice(64 * h, 64 * h + 64)
                            yo = psY[:, 64 * h:64 * h + 64]
                            mm(yo, ATb[h][0][:, 128:256], Ub[:, hcs], False, False, (ATb[h][1], r_Ub), (rS[2],))
                            mm(yo, ATb[h][0][:, 384:512], Vtok[:, gc, hcs], False, h == 1, (ATb[h][1], r_Vtok), (rS[2],))
                        mm(Pp, Btok[:, ci, :], Ub[:], True, False, (r_Btok, r_Ub), (rS[3],))
                        mm(Pp, Ktok[:, ci, :], Vtok[:, gc, :], False, True, (r_Ktok, r_Vtok), (rS[3],))
                        stt("dve", Hm[:], Pp, gam[:, ci:ci + 1], bdm[:], ALU.mult, ALU.mult, (rS[3], r_gam, r_bdm), (r_Hm,))
                        stt("dve", Hb[:], Hf[:], gam[:, ci:ci + 1], Hm[:], ALU.mult, ALU.add, (r_Hf, r_gam, r_Hm), (r_Hb,))
                        stt("dve", Hf[:], Hf[:], gam[:, ci:ci + 1], Hm[:], ALU.mult, ALU.add, (r_Hf, r_gam, r_Hm), (r_Hf,))
                        if sq >= 0 and oi == len(order) - 1:
                            g.dma("sp", hst[sq, d, hp, :, :], Hf[:], (r_Hf,), ())
                        if d == 0:
                            cp("dve", yf[:, gc, :], Yp, (rS[2],), (r_yf,))
                            continue
                        tp = tailcnt[0] % 2
                        tailcnt[0] += 1
                        ys_, rys_ = YS[tp]
                        tt("dve", ys_[:], Yp, yf[:, gc, :], ALU.add, (rS[2], r_yf), (rys_,))
                        if len(pending) == 2:
                            for _ in pending.pop(0):
                                pass
                        if pending:
                            next(pending[-1])

                        def tail(tp=tp, gc=gc):
                            ys_, rys_ = YS[tp]
                            yq_, ryq_ = YQ[tp]
                            yn_, ryn_ = YN[tp]
                            zp_, rzp_ = ZP[tp]
                            s4, rs4 = ST4[tp]
                            g.op("dve", lambda e: e.reduce_sum(s4[:, 0:2], ys_[:].rearrange("p (h i) -> p h i", h=2), AX.X),
                                 (rys_,), (rs4,))
                            tt("dve", yq_[:], ys_[:], ys_[:], ALU.mult, (rys_,), (ryq_,))
                            g.op("dve", lambda e: e.reduce_sum(s4[:, 2:4], yq_[:].rearrange("p (h i) -> p h i", h=2), AX.X),
                                 (ryq_,), (rs4,))
                            ts("dve", s4[:, 0:4], s4[:, 0:4], 1.0 / 64, None, ALU.mult, ALU.bypass, (rs4,), (rs4,))
                            tt("dve", s4[:, 4:6], s4[:, 0:2], s4[:, 0:2], ALU.mult, (rs4,), (rs4,))
                            tt("dve", s4[:, 4:6], s4[:, 2:4], s4[:, 4:6], ALU.subtract, (rs4,), (rs4,))
                            act(s4[:, 6:8], s4[:, 4:6], AF.Sqrt, (rs4,), (rs4,), bias=GN_EPS)
                            g.op("dve", lambda e: e.reciprocal(s4[:, 6:8], s4[:, 6:8]), (rs4,), (rs4,))
                            for h in range(2):
                                hcs = slice(64 * h, 64 * h + 64)
                                ts("dve", yn_[:, hcs], ys_[:, hcs], s4[:, h:h + 1], s4[:, 6 + h:7 + h], ALU.subtract, ALU.mult,
                                   (rys_, rs4), (ryn_,))
                            tt("pool", yn_[:], yn_[:], lnxg[:], ALU.mult, (ryn_, r_lnxg), (ryn_,))
                            tt("pool", yn_[:], yn_[:], lnxb[:], ALU.add, (ryn_, r_lnxb), (ryn_,))
                            for h in range(2):
                                hcs = slice(64 * h, 64 * h + 64)
                                ts("pool", yq_[:, hcs], Vtok[:, gc, hcs], sbon[:, gc, h:h + 1], None, ALU.mult, ALU.bypass,
                                   (r_Vtok, r_sbon), (ryq_,))
                            tt("pool", zp_[:], yq_[:], yn_[:], ALU.add, (ryq_, ryn_), (rzp_,))
                            yield
                            p_, rp = npt()
                            tr(p_[:, 0:128], zp_[:], ident[:], (rzp_, r_ident), (rp,))
                            gcs = slice(gc * 128, gc * 128 + 128)
                            tt("dve", zrow[:, gcs], p_[:, 0:128], grT[:, gcs], ALU.mult, (rp, r_grT), (r_zrow,))
                        pending.append(tail())
                for _ in nxt:
                    pass
        while pending:
            for _ in pending.pop(0):
                pass
        g.dma("sp", ZT[hp, :, :], zrow[:], (r_zrow,), ())

    if stop == 4:
        return finish()
    g.barrier()
    arena["off"] = PERSIST
    rr["n"] = 4
    wpf, r_wpf = sb([128, 4, 1024], BF16, "wpf")
    wpr, r_wpr = sb([128, 8, 1024], BF16, "wpr")
    wo, r_wo = sb([128, 8, 1024], BF16, "wo")
    wst = [sb([128, 1024], F32, "wst%d" % i) for i in range(2)]
    n_ = 0
    for (dst, rdst, src, nk) in ((wpf, r_wpf, wpf_d, 4), (wpr, r_wpr, wpr_d, 8), (wo, r_wo, wo_d, 8)):
        for kc in range(nk):
            w_, rw = wst[n_ % 2]
            ld(w_[:], src[kc * 128:(kc + 1) * 128, :], rw, q="pool" if n_ % 2 else "sp")
            cp("pool" if n_ % 2 else "dve", dst[:, kc, :], w_[:], (rw,), (rdst,))
            n_ += 1
    FZb, r_FZb = sb([128, 4, 512], BF16, "FZb")
    ZTb, r_ZTb = sb([128, 8, 512], BF16, "ZTb")
    MG, r_MG = sb([128, 16, 512], BF16, "MG")
    mT, r_mT = sb([128, 8, 512], BF16, "mT")
    u1, r_u1 = sb([128, 512], F32, "u1")
    u2, r_u2 = sb([128, 512], F32, "u2")
    xt5 = [sb([128, 1024], F32, "xt5_%d" % i) for i in range(2)]
    o1, r_o1 = sb([128, 1024], F32, "o1")
    xs, r_xs = sb([128, 1024], F32, "xs")
    yo_ = [sb([128, 1024], F32, "yo%d" % i) for i in range(2)]
    junk5, r_junk5 = sb([128, 1024], BF16, "junk5")
    ss5 = [sb([128, 1], F32, "ss5_%d" % i) for i in range(2)]
    for tb in range(9):
        bs = slice(tb * 512, tb * 512 + 512)
        ld(FZb[:], FZ.ap()[:, :, bs].rearrange("g p t -> p g t"), r_FZb)
        ld(ZTb[:], ZT.ap()[:, :, bs].rearrange("g p t -> p g t"), r_ZTb, q="pool")
        ld(MG[:], UT.ap()[41:57, :, bs].rearrange("g p t -> p g t"), r_MG)
        for n in range(8):
            ns = slice(n * 128, n * 128 + 128)
            pf, rpf = nps()
            for kc in range(4):
                mm(pf[:, :], wpf[:, kc, ns], FZb[:, kc, :], kc == 0, kc == 3, (r_wpf, r_FZb), (rpf,))
            pr, rpr = nps()
            for kc in range(8):
                mm(pr[:, :], wpr[:, kc, ns], ZTb[:, kc, :], kc == 0, kc == 7, (r_wpr, r_ZTb), (rpr,))
            tt("dve", u1[:], pf[:, :], MG[:, n, :], ALU.mult, (rpf, r_MG), (r_u1,))
            tt("dve", u2[:], pr[:, :], MG[:, 8 + n, :], ALU.mult, (rpr, r_MG), (r_u2,))
            tt("pool", mT[:, n, :], u1[:], u2[:], ALU.add, (r_u1, r_u2), (r_mT,))
        for t4 in range(4):
            i = tb * 4 + t4
            which = 0 if i < 32 else 1
            x_, rx = xt5[i % 2]
            ld(x_[:], x_all[i * 128:(i + 1) * 128, :], rx, q="pool")
            for half in range(2):
                hs = slice(half * 512, half * 512 + 512)
                po, rpo = nps()
                for kc in range(8):
                    mm(po[:, :], mT[:, kc, t4 * 128:(t4 + 1) * 128], wo[:, kc, hs], kc == 0, kc == 7, (r_mT, r_wo), (rpo,))
                tt("dve", o1[:, hs], po[:, :], Gbc[:, which, hs], ALU.mult, (rpo, r_Gbc), (r_o1,))
            tt("pool", xs[:], o1[:], x_[:], ALU.add, (r_o1, rx), (r_xs,))
            s_, rs_ = ss5[i % 2]
            act(junk5[:], xs[:], AF.Square, (r_xs,), (r_junk5, rs_), accum=s_[:])
            act(s_[:], s_[:], AF.Sqrt, (rs_,), (rs_,), bias=RMS_EPS, scale=1.0 / 1024)
            g.op("dve", lambda e: e.reciprocal(s_[:], s_[:]), (rs_,), (rs_,))
            y_, ry = yo_[i % 2]
            stt("dve", y_[:], xs[:], s_[:, 0:1], fgb[:], ALU.mult, ALU.mult, (r_xs, rs_, r_fgb), (ry,))
            g.dma("sp", y_all[i * 128:(i + 1) * 128, :], y_[:], (ry,), ())
    _CACHE['cnt'] = (dict(g.cnt), list(g.dcnt))
    for i in range(len(g.dcnt)):
        if g.dcnt[i]:
            nc.sync.wait_ge(g.dsem[i], g.dcnt[i])
    return nc


_CACHE = {}


def _consts():
    if "c" in _CACHE:
        return _CACHE["c"]
    c = {}
    c["ident"] = np.eye(128, dtype=np.float32).astype(NPBF)
    bo = np.zeros((128, 128), np.float32)
    bo[:64, :64] = 1
    bo[64:, 64:] = 1
    c["bones"] = bo.astype(NPBF)
    c["bdm"] = bo.copy()
    b2 = np.zeros((128, 2), np.float32)
    b2[:64, 0] = 1
    b2[64:, 1] = 1
    c["bones2"] = b2.astype(NPBF)
    s = np.arange(128)[:, None]
    t = np.arange(128)[None, :]
    m4 = np.zeros((2, 128, 512), np.float32)
    for d, (st, inc) in enumerate((((s < t), (s <= t)), ((s > t), (s >= t)))):
        m4[d, :, 0:128] = st
        m4[d, :, 128:256] = inc
        m4[d, :, 256:384] = st
        m4[d, :, 384:512] = inc
    c["mask4"] = m4.astype(NPBF)
    mT = np.zeros((2, 128, 128), np.float32)
    mT[0] = (t < s)
    mT[1] = (t > s)
    c["maskT"] = mT.astype(NPBF)
    c["I2"] = np.concatenate([np.eye(128)] * 4, 1).astype(np.float32).astype(NPBF)
    rI = np.arange(128)[:, None]
    cI = np.arange(128)[None, :]
    mblk = np.zeros((3, 2, 128, 256), np.float32)
    for d in range(2):
        for half in range(2):
            tI, sI = (rI, cI) if half == 0 else (cI, rI)
            strict = (sI < tI) if d == 0 else (sI > tI)
            b32 = (tI // 32) == (sI // 32)
            b64 = (tI // 64) == (sI // 64)
            hs_ = slice(half * 128, half * 128 + 128)
            mblk[0, d, :, hs_] = strict & b32
            mblk[1, d, :, hs_] = strict & b64 & ~b32
            mblk[2, d, :, hs_] = strict & ~b64
    c["mblk"] = mblk.astype(NPBF)
    rm = np.ones((128, 1024), np.float32)
    rm[:, ::128] = 0
    c["rmask"] = rm
    sw = np.zeros((2, 2, 128), np.float32)
    sw[0, 0, :] = 1
    sw[1, 1, :] = 1
    c["selw"] = sw
    n = 4096
    tt_ = np.arange(n, dtype=np.int64)
    ph = (tt_[:, None] * tt_[None, :]) % n
    ang = 2 * np.pi * ph.astype(np.float64) / n
    sc = 1.0 / np.sqrt(n * 128.0)
    tabs = np.stack([np.cos(ang) * sc, np.sin(ang) * sc], 0).astype(np.float32)
    tabs = tabs.reshape(2, 32, 128, 8, 512).transpose(0, 3, 2, 1, 4)
    c["dftL"] = np.ascontiguousarray(tabs).reshape(2, 8, 128, 32 * 512).astype(NPBF)
    n2 = 256
    t2 = np.arange(n2, dtype=np.int64)
    ang2 = 2 * np.pi * ((t2[:, None] * t2[None, :]) % n2).astype(np.float64) / n2
    sc2 = 1.0 / np.sqrt(n2 * 128.0)
    tb2 = np.stack([np.cos(ang2) * sc2, np.sin(ang2) * sc2], 0).astype(np.float32)
    tb2 = tb2.reshape(2, 2, 128, 256).transpose(2, 0, 1, 3)
    c["dftC"] = np.ascontiguousarray(tb2).astype(NPBF)
    cidx = np.arange(128, dtype=np.int64)
    ang3 = 2 * np.pi * ((cidx[:, None] * cidx[None, :]) % 128).astype(np.float64) / 128
    c128 = np.stack([np.cos(ang3), -np.sin(ang3)], 1).astype(np.float32)
    c["c128"] = np.ascontiguousarray(c128).astype(NPBF)
    _CACHE["c"] = c
    return c


def kernel(x_prompt, x_sample, state_rwkv, c, c_ctx, norm_g, w_ada, b_ada, w_in, mu_shift, w0, w_up, a0, a_up,
           k_k, k_a, r_k, lnx_g, lnx_b, w_proj_f, w_proj_r, w_out, final_g):
    f = lambda a: np.ascontiguousarray(np.asarray(a, dtype=np.float32))
    x_prompt, x_sample, state_rwkv, c, c_ctx = f(x_prompt), f(x_sample), f(state_rwkv), f(c), f(c_ctx)
    cst = _consts()
    col = lambda v: np.ascontiguousarray(f(v).reshape(8, 128).T)
    pcols = np.stack([col(k_k[0]), col(k_a[0]), col(r_k[0].reshape(-1)), col(w0[0, 0]), col(w0[0, 1]),
                      col(a0[0, 0]), col(a0[0, 1])], 1)
    mu = f(mu_shift[0])
    muf = np.ascontiguousarray(mu.reshape(25, 128).T)
    sidx = np.arange(3200).reshape(25, 128).T
    mu6 = np.zeros((128, 25, 6), np.float32)
    for m in range(4):
        mu6[:, :, m] = np.where(sidx % 4 == m, muf, 0)
    mu6[:, :, 4] = np.where(sidx % 2 == 0, muf, 0)
    mu6[:, :, 5] = np.where(sidx % 2 == 1, muf, 0)
    rep = lambda v: np.ascontiguousarray(np.broadcast_to(f(v).reshape(1, -1), (128, f(v).size)))
    shared = dict(cst)
    shared.update({
        "w_ada": f(w_ada[0]), "b_ada2": np.ascontiguousarray(np.broadcast_to(f(b_ada[0])[None], (2, 3072))),
        "ngb": rep(norm_g[0]), "fgb": rep(final_g), "w_in": f(w_in[0]), "pcols": np.ascontiguousarray(pcols),
        "mu6": mu6, "muf": muf, "lnxg": rep(lnx_g[0]), "lnxb": rep(lnx_b[0]),
        "wup": np.ascontiguousarray(f(w_up[0]).transpose(1, 0, 2)), "aup": np.ascontiguousarray(f(a_up[0]).transpose(1, 0, 2)),
        "w_proj_f": f(w_proj_f[0]), "w_proj_r": f(w_proj_r[0]), "w_out": f(w_out[0]),
    })
    in_maps = []
    for core in range(8):
        b = core // 4
        m = dict(shared)
        m["x_all"] = np.ascontiguousarray(np.concatenate([x_sample[b], x_prompt[2 * core], x_prompt[2 * core + 1]], 0))
        ccv = np.stack([c[b].reshape(8, 128).T, c_ctx.reshape(8, 128).T], -1)
        m["cc"] = np.ascontiguousarray(ccv)
        st = state_rwkv[b, 0]
        h0 = np.zeros((2, 8, 128, 128), np.float32)
        for d in range(2):
            for hp in range(8):
                for h in range(2):
                    h0[d, hp, 64 * h:64 * h + 64, 64 * h:64 * h + 64] = st[d, 2 * hp + h].T
        m["h0bd"] = h0
        in_maps.append(m)
    if _CACHE.get("prep_only"):
        return in_maps
    if "nc" not in _CACHE:
        _CACHE["nc"] = build()
    res = run_bass_kernel_spmd(_CACHE["nc"], in_maps, core_ids=list(range(8)))
    y_prompt = np.zeros((16, 256, 1024), np.float32)
    y_sample = np.zeros((2, 4096, 1024), np.float32)
    new_state = np.zeros((16, 1, 2, 16, 64, 64), np.float32)
    for core in range(8):
        r = res.results[core]
        ya = np.asarray(r["y_all"])
        q = core % 4
        y_sample[core // 4, q * 1024:(q + 1) * 1024] = ya[q * 1024:(q + 1) * 1024]
        y_prompt[2 * core] = ya[4096:4352]
        y_prompt[2 * core + 1] = ya[4352:4608]
        hs_ = np.asarray(r["hst"])
        for sq in range(2):
            for d in range(2):
                for hp in range(8):
                    for h in range(2):
                        new_state[2 * core + sq, 0, d, 2 * hp + h] = hs_[sq, d, hp, 64 * h:64 * h + 64, 64 * h:64 * h + 64].T
    return (y_prompt, y_sample, new_state)
```

```python
import numpy as np
import ml_dtypes
import concourse.bass as bass
import concourse.mybir as mybir
from concourse.bass_utils import run_bass_kernel_spmd

F32, BF16 = mybir.dt.float32, mybir.dt.bfloat16
AF = mybir.ActivationFunctionType
ALU = mybir.AluOpType
AX = mybir.AxisListType
NPBF = ml_dtypes.bfloat16

NT = 4608
NCHUNK = 36
C0 = 0.6065306597126334
RMS_EPS = 1e-6
GN_EPS = 64e-5


class Res:
    __slots__ = ("w", "r")

    def __init__(self):
        self.w = None
        self.r = {}


class G:
    def __init__(self, nc):
        self.nc = nc
        self.E = {"pe": nc.tensor, "act": nc.scalar, "dve": nc.vector, "pool": nc.gpsimd, "sp": nc.sync}
        self.sem = {e: nc.alloc_semaphore("s_" + e) for e in ("pe", "act", "dve", "pool")}
        self.cnt = {e: 0 for e in self.sem}
        self.NDS = 12
        self.dsem = [nc.alloc_semaphore("d%d" % i) for i in range(2 * self.NDS)]
        self.dcnt = [0] * (2 * self.NDS)
        self.di = 0
        self.di_sw = 0
        self.seen = {e: {} for e in self.E}
        self.mute = False

    def _semof(self, k):
        return self.sem[k] if isinstance(k, str) else self.dsem[k]

    def _wait(self, e, toks):
        best = {}
        for k, v in toks:
            if k == e and e == "pe":
                continue
            if self.seen[e].get(k, 0) >= v:
                continue
            if best.get(k, 0) < v:
                best[k] = v
        for k, v in best.items():
            self.E[e].wait_ge(self._semof(k), v)
            self.seen[e][k] = v

    @staticmethod
    def _deps(reads, writes):
        toks = []
        for r in reads:
            if r.w:
                toks.append(r.w)
        for w in writes:
            if w.w:
                toks.append(w.w)
            toks.extend(w.r.items())
        return toks

    @staticmethod
    def _upd(tok, reads, writes):
        for r in reads:
            if r.r.get(tok[0], 0) < tok[1]:
                r.r[tok[0]] = tok[1]
        for w in writes:
            w.w = tok
            w.r = {}

    def op(self, e, fn, reads=(), writes=()):
        if self.mute:
            return
        self._wait(e, self._deps(reads, writes))
        ins = fn(self.E[e])
        self.cnt[e] += 1
        tok = (e, self.cnt[e])
        ins.then_inc(self.sem[e], 1)
        self._upd(tok, reads, writes)

    def dma(self, q, out, in_, reads=(), writes=()):
        if self.mute:
            return
        self._wait(q, self._deps(reads, writes))
        if q == "pool":
            i = self.NDS + self.di_sw
            self.di_sw = (self.di_sw + 1) % self.NDS
        else:
            i = self.di
            self.di = (self.di + 1) % self.NDS
        if self.dcnt[i]:
            self._wait(q, [(i, self.dcnt[i])])
        ins = self.E[q].dma_start(out=out, in_=in_)
        self.dcnt[i] += 16
        ins.then_inc(self.dsem[i], 16)
        self._upd((i, self.dcnt[i]), reads, writes)

    def barrier(self):
        toks = [(e, c) for e, c in self.cnt.items() if c] + [(i, c) for i, c in enumerate(self.dcnt) if c]
        for e in self.E:
            self._wait(e, [t for t in toks if t[0] != e])


def build(stop=99, debug=False, nhp=8, only3=False, stop3=99):
    nc = bass.Bass("TRN2", target_bir_lowering=False)
    g = G(nc)

    def din(name, shape, dt=F32):
        return nc.dram_tensor(name, list(shape), dt, kind="ExternalInput")

    x_all = din("x_all", [NT, 1024])
    cc = din("cc", [128, 8, 2])
    w_ada = din("w_ada", [1024, 3072])
    b_ada2 = din("b_ada2", [2, 3072])
    selw = din("selw", [2, 2, 128])
    ngb_d = din("ngb", [128, 1024])
    fgb_d = din("fgb", [128, 1024])
    w_in = din("w_in", [1024, 7296])
    ident_d = din("ident", [128, 128], BF16)
    bones_d = din("bones", [128, 128], BF16)
    bones2_d = din("bones2", [128, 2], BF16)
    mask4_d = din("mask4", [2, 128, 512], BF16)
    maskT_d = din("maskT", [2, 128, 128], BF16)
    bdm_d = din("bdm", [128, 128])
    I2_d = din("I2", [128, 512], BF16)
    mblk_d = din("mblk", [3, 2, 128, 256], BF16)
    rmask_d = din("rmask", [128, 1024])
    pcols_d = din("pcols", [128, 7, 8])
    mu6_d = din("mu6", [128, 25, 6])
    muf_d = din("muf", [128, 25])
    lnxg_d = din("lnxg", [128, 1024])
    lnxb_d = din("lnxb", [128, 1024])
    wup_d = din("wup", [64, 2, 1024])
    aup_d = din("aup", [64, 2, 1024])
    h0_d = din("h0bd", [2, 8, 128, 128])
    dft_d = din("dftL", [2, 8, 128, 32 * 512], BF16)
    dftc_d = din("dftC", [128, 2, 2, 256], BF16)
    c128_d = din("c128", [128, 2, 128], BF16)
    wpf_d = din("w_proj_f", [512, 1024])
    wpr_d = din("w_proj_r", [1024, 1024])
    wo_d = din("w_out", [1024, 1024])

    y_all = nc.dram_tensor("y_all", [NT, 1024], F32, kind="ExternalOutput")
    hst = nc.dram_tensor("hst", [2, 2, 8, 128, 128], F32, kind="ExternalOutput")

    sk = "ExternalOutput" if debug else "Internal"
    UT = nc.dram_tensor("UT", [57, 128, NT], BF16, kind="ExternalInput" if only3 else sk)
    XF = nc.dram_tensor("XF", [NT, 512], BF16, kind=sk)
    FZ = nc.dram_tensor("FZ", [4, 128, NT], BF16, kind=sk)
    ZT = nc.dram_tensor("ZT", [8, 128, NT], BF16, kind=sk)

    DBG = nc.dram_tensor("DBG", [128, 8 * NT], BF16, kind="ExternalOutput") if debug else None
    DBGF = nc.dram_tensor("DBGF", [128, 8192], F32, kind="ExternalOutput") if debug else None

    def dump(off, t_ap, res, n, f32=False):
        tgt = DBGF if f32 else DBG
        g.dma("sp", tgt[:, off:off + n], t_ap, (res,), ())

    def finish():
        for i in range(len(g.dcnt)):
            if g.dcnt[i]:
                nc.sync.wait_ge(g.dsem[i], g.dcnt[i])
        _CACHE['cnt'] = (dict(g.cnt), list(g.dcnt))
        return nc

    ABASE = 16384 + 128
    arena = {"off": ABASE, "n": 0}

    def sb(shape, dt, name=None):
        nbytes = int(np.prod(shape[1:])) * (4 if dt == F32 else 2)
        nbytes = (nbytes + 63) // 64 * 64
        arena["n"] += 1
        t = nc.alloc_sbuf_tensor_at("t%d_%s" % (arena["n"], name or ""), list(shape), dt, offset=arena["off"])
        arena["off"] += nbytes
        assert arena["off"] <= ABASE + 208 * 1024, arena["off"]
        return t, Res()

    ps = []
    for i in range(6):
        ps.append((nc.alloc_psum_tensor("ps%d" % i, [128, 512], F32), Res()))
    pt = []
    for i in range(2):
        pt.append((nc.alloc_psum_tensor("pt%d" % i, [128, 1024], BF16), Res()))

    rr = {"ps": 0, "pt": 0, "q": 0}

    rr["n"] = 4

    def nps():
        rr["ps"] = (rr["ps"] + 1) % rr["n"]
        return ps[rr["ps"]]

    def npt():
        rr["pt"] = (rr["pt"] + 1) % 2
        return pt[rr["pt"]]

    def mm(out, lhsT, rhs, start, stop, reads, writes):
        g.op("pe", lambda e: e.matmul(out, lhsT, rhs, start=start, stop=stop), reads, writes)

    def tr(out, in_, ident, reads, writes):
        g.op("pe", lambda e: e.transpose(out, in_, ident), reads, writes)

    def act(out, in_, func, reads, writes, bias=0.0, scale=1.0, accum=None):
        if accum is None:
            g.op("act", lambda e: e.activation(out, in_, func, bias=bias, scale=scale), reads, writes)
        else:
            g.op("act", lambda e: e.activation(out, in_, func, bias=bias, scale=scale, accum_out=accum), reads, writes)

    def tt(eng, out, a, b, op, reads, writes):
        g.op(eng, lambda e: e.tensor_tensor(out, a, b, op), reads, writes)

    def ts(eng, out, a, s1, s2, op0, op1, reads, writes):
        if s2 is None:
            s2, op1 = 0.0, ALU.add
        g.op(eng, lambda e: e.tensor_scalar(out, a, s1, s2, op0, op1), reads, writes)

    def stt(eng, out, a, s, b, op0, op1, reads, writes):
        eng = "dve"
        g.op(eng, lambda e: e.scalar_tensor_tensor(out, a, s, b, op0, op1), reads, writes)

    def cp(eng, out, in_, reads, writes):
        if eng == "act":
            g.op("act", lambda e: e.copy(out, in_), reads, writes)
        else:
            g.op(eng, lambda e: e.tensor_copy(out, in_), reads, writes)

    def ld(out, in_, res, q="sp"):
        g.dma(q, out, in_, (), (res,))

    ident, r_ident = sb([128, 128], BF16, "ident")
    bones, r_bones = sb([128, 128], BF16, "bones")
    bones2, r_bones2 = sb([128, 2], BF16, "bones2")
    mask4, r_mask4 = sb([128, 2, 512], BF16, "mask4")
    maskT, r_maskT = sb([128, 2, 128], BF16, "maskT")
    bdm, r_bdm = sb([128, 128], F32, "bdm")
    pcols, r_pcols = sb([128, 7, 8], F32, "pcols")
    oka, r_oka = sb([128, 8], F32, "oka")
    Gbc, r_Gbc = sb([128, 2, 1024], F32, "Gbc")
    fgb, r_fgb = sb([128, 1024], F32, "fgb")
    ld(ident[:], ident_d[:, :], r_ident)
    ld(bones[:], bones_d[:, :], r_bones)
    ld(bones2[:], bones2_d[:, :], r_bones2)
    for d in range(2):
        ld(mask4[:, d, :], mask4_d[d, :, :], r_mask4)
        ld(maskT[:, d, :], maskT_d[d, :, :], r_maskT)
    ld(bdm[:], bdm_d[:, :], r_bdm)
    ld(pcols[:], pcols_d[:, :, :], r_pcols)
    ld(fgb[:], fgb_d[:, :], r_fgb)
    ts("dve", oka[:], pcols[:, 1, :], -1.0, 1.0, ALU.mult, ALU.add, (r_pcols,), (r_oka,))
    PERSIST = arena["off"]

    g.mute = only3
    hT, r_hT = sb([128, 8, NT], BF16, "hT")
    P1BASE = arena["off"]
    ccs, r_ccs = sb([128, 8, 2], F32, "ccs")
    ld(ccs[:], cc[:, :, :], r_ccs)
    act(ccs[:], ccs[:], AF.Silu, (r_ccs,), (r_ccs,))
    wad = [sb([128, 8, 512], F32, "wad%d" % i) for i in range(2)]
    mrows, r_mrows = sb([2, 3072], F32, "mrows")
    bad, r_bad = sb([2, 3072], F32, "bad")
    selw_s, r_selw = sb([2, 2, 128], F32, "selw")
    ngb, r_ngb = sb([128, 1024], F32, "ngb")
    Abc, r_Abc = sb([128, 2, 1024], F32, "Abc")
    Bbc, r_Bbc = sb([128, 2, 1024], F32, "Bbc")
    ld(bad[:], b_ada2[:, :], r_bad)
    ld(selw_s[:], selw[:, :, :], r_selw)
    ld(ngb[:], ngb_d[:, :], r_ngb)
    w_ada_v = w_ada.ap().rearrange("(kc p) n -> p kc n", p=128)
    for n in range(6):
        wt, rw = wad[n % 2]
        ld(wt[:], w_ada_v[:, :, n * 512:(n + 1) * 512], rw, q="sp" if n % 2 == 0 else "pool")
        p_, rp = nps()
        for kc in range(8):
            mm(p_[0:2, :], ccs[:, kc, :], wt[:, kc, :], kc == 0, kc == 7, (r_ccs, rw), (rp,))
        tt("dve", mrows[:, n * 512:(n + 1) * 512], p_[0:2, :], bad[:, n * 512:(n + 1) * 512], ALU.add,
           (rp, r_bad), (r_mrows,))
    for which in range(2):
        for kind in range(3):
            for half in range(2):
                p_, rp = nps()
                c0 = kind * 1024 + half * 512
                mm(p_[:, :], selw_s[:, which, :], mrows[:, c0:c0 + 512], True, True, (r_selw, r_mrows), (rp,))
                hs = slice(half * 512, half * 512 + 512)
                if kind == 0:
                    cp("act", Bbc[:, which, hs], p_[:, :], (rp,), (r_Bbc,))
                elif kind == 1:
                    stt("dve", Abc[:, which, hs], p_[:, :], 1.0, ngb[:, hs], ALU.add, ALU.mult, (rp, r_ngb), (r_Abc,))
                else:
                    cp("act", Gbc[:, which, hs], p_[:, :], (rp,), (r_Gbc,))

    xt = [sb([128, 1024], F32, "xt%d" % i) for i in range(2)]
    junk, r_junk = sb([128, 1024], BF16, "junk")
    t1, r_t1 = sb([128, 1024], F32, "t1")
    hb = [sb([128, 1024], BF16, "hb%d" % i) for i in range(2)]
    ssq = [sb([128, 1], F32, "ss%d" % i) for i in range(2)]
    rst = [sb([128, 1], F32, "rst%d" % i) for i in range(2)]
    for i in range(NCHUNK):
        which = 0 if i < 32 else 1
        x_, rx = xt[i % 2]
        ld(x_[:], x_all[i * 128:(i + 1) * 128, :], rx, q="sp" if i % 2 == 0 else "pool")
        s_, rs_ = ssq[i % 2]
        q_, rq = rst[i % 2]
        act(junk[:], x_[:], AF.Square, (rx,), (r_junk, rs_), accum=s_[:])
        act(q_[:], s_[:], AF.Sqrt, (rs_,), (rq,), bias=RMS_EPS, scale=1.0 / 1024)
        g.op("dve", lambda e: e.reciprocal(q_[:], q_[:]), (rq,), (rq,))
        stt("dve", t1[:], x_[:], q_[:, 0:1], Abc[:, which, :], ALU.mult, ALU.mult, (rx, rq, r_Abc), (r_t1,))
        h_, rh = hb[i % 2]
        tt("pool", h_[:], t1[:], Bbc[:, which, :], ALU.add, (r_t1, r_Bbc), (rh,))
        p_, rp = npt()
        for kc in range(8):
            tr(p_[:, kc * 128:(kc + 1) * 128], h_[:, kc * 128:(kc + 1) * 128], ident[:], (rh, r_ident), (rp,))
        cp("act" if i % 2 else "dve", hT[:, :, i * 128:(i + 1) * 128],
           p_[:, :].rearrange("p (k t) -> p k t", k=8), (rp,), (r_hT,))

    if stop == 1:
        if debug:
            g.dma("sp", DBG[:, :], hT[:].rearrange("p k t -> p (k t)"), (r_hT,), ())
        return finish()
    g.barrier()
    arena["off"] = P1BASE
    w_in_v = w_in.ap().rearrange("(kc p) n -> p kc n", p=128)
    wxf32, r_wxf32 = sb([128, 8, 512], F32, "wxf32")
    wxf, r_wxf = sb([128, 8, 512], BF16, "wxf")
    ld(wxf32[:], w_in_v[:, :, 0:512], r_wxf32)
    cp("dve", wxf[:], wxf32[:], (r_wxf32,), (r_wxf,))
    xfst = [sb([128, 512], BF16, "xfst%d" % i) for i in range(2)]
    for i in range(NCHUNK):
        p_, rp = nps()
        for kc in range(8):
            mm(p_[:, :], hT[:, kc, i * 128:(i + 1) * 128], wxf[:, kc, :], kc == 0, kc == 7, (r_hT, r_wxf), (rp,))
        s_, rs_ = xfst[i % 2]
        cp("act" if i % 2 else "dve", s_[:], p_[:, :], (rp,), (rs_,))
        g.dma("sp", XF[i * 128:(i + 1) * 128, :], s_[:], (rs_,), ())
    w32 = [sb([128, 8, 128], F32, "w32_%d" % i) for i in range(2)]
    wb = [sb([128, 8, 128], BF16, "wb%d" % i) for i in range(2)]
    stage = [sb([128, NT], BF16, "stage%d" % i) for i in range(2)]
    for ct in range(4, 57):
        k = ct % 2
        w_, rw = w32[k]
        ld(w_[:], w_in_v[:, :, ct * 128:(ct + 1) * 128], rw, q="pool" if k else "sp")
        b_, rb = wb[k]
        cp("pool" if k else "dve", b_[:], w_[:], (rw,), (rb,))
        if ct < 8 or 33 <= ct < 41:
            fn = AF.Silu
        elif ct >= 41:
            fn = AF.Sigmoid
        else:
            fn = AF.Copy
        st_, rs_ = stage[k]
        for blk in range(9):
            p_, rp = nps()
            cs_ = slice(blk * 512, blk * 512 + 512)
            for kc in range(8):
                mm(p_[:, :], b_[:, kc, :], hT[:, kc, cs_], kc == 0, kc == 7, (rb, r_hT), (rp,))
            if fn == AF.Copy:
                cp("dve" if blk % 2 == 0 else "act", st_[:, cs_], p_[:, :], (rp,), (rs_,))
            else:
                act(st_[:, cs_], p_[:, :], fn, (rp,), (rs_,))
        g.dma("sp", UT[ct, :, :], st_[:], (rs_,), ())

    if stop == 2:
        return finish()
    g.barrier()
    arena["off"] = PERSIST
    xftok, r_xftok = sb([128, NCHUNK, 512], BF16, "xftok")
    gfT, r_gfT = sb([128, 4, NT], BF16, "gfT")
    ld(xftok[:], XF.ap().rearrange("(i p) c -> p i c", p=128), r_xftok)
    for gi in range(4):
        ld(gfT[:, gi, :], UT[4 + gi, :, :], r_gfT, q="pool")
    tabs = [sb([128, 32, 512], BF16, "tab%d" % i) for i in range(2)]
    ctab, r_ctab = sb([128, 2, 2, 256], BF16, "ctab")
    c128, r_c128 = sb([128, 2, 128], BF16, "c128")
    ld(ctab[:], dftc_d[:, :, :, :], r_ctab)
    ld(c128[:], c128_d[:, :, :], r_c128)
    Pb = [[sb([128, 512], BF16, "P%d_%d" % (a, b)) for b in range(4)] for a in range(2)]
    fzst = [sb([128, 512], BF16, "fzst%d" % i) for i in range(2)]
    nld = 0

    def fourier_tail(ncol, col0):
        for gi in range(4):
            p_, rp = ps[4 + gi % 2]
            mm(p_[:, 0:ncol], c128[:, 0, :], Pb[0][gi][0][:, 0:ncol], True, False, (r_c128, Pb[0][gi][1]), (rp,))
            mm(p_[:, 0:ncol], c128[:, 1, :], Pb[1][gi][0][:, 0:ncol], False, True, (r_c128, Pb[1][gi][1]), (rp,))
            s_, rs_ = fzst[gi % 2]
            tt("dve", s_[:, 0:ncol], p_[:, 0:ncol], gfT[:, gi, col0:col0 + ncol], ALU.mult, (rp, r_gfT), (rs_,))
            g.dma("sp", FZ[gi, :, col0:col0 + ncol], s_[:, 0:ncol], (rs_,), ())

    for pb in range(8):
        for cs_i in range(2):
            tb_, rt_ = tabs[nld % 2]
            ld(tb_[:], dft_d[cs_i, pb, :, :].rearrange("p (t f) -> p t f", t=32), rt_, q="pool" if nld % 2 else "sp")
            nld += 1
            for tk in range(32):
                for gi in range(4):
                    mm(ps[gi][0][:, :], xftok[:, tk, gi * 128:(gi + 1) * 128], tb_[:, tk, :], tk == 0, tk == 31,
                       (r_xftok, rt_), (ps[gi][1],))
            for gi in range(4):
                cp("act" if gi % 2 else "dve", Pb[cs_i][gi][0][:], ps[gi][0][:, :], (ps[gi][1],), (Pb[cs_i][gi][1],))
        fourier_tail(512, pb * 512)
    for sq in range(2):
        for cs_i in range(2):
            for gi in range(4):
                for tk in range(2):
                    mm(ps[gi][0][:, 0:256], xftok[:, 32 + 2 * sq + tk, gi * 128:(gi + 1) * 128], ctab[:, cs_i, tk, :],
                       tk == 0, tk == 1, (r_xftok, r_ctab), (ps[gi][1],))
                cp("act" if gi % 2 else "dve", Pb[cs_i][gi][0][:, 0:256], ps[gi][0][:, 0:256], (ps[gi][1],),
                   (Pb[cs_i][gi][1],))
        fourier_tail(256, 4096 + 256 * sq)

    if stop == 3:
        return finish()
    g.mute = False
    g.barrier()
    arena["off"] = PERSIST
    rmask, r_rmask = sb([128, 512], F32, "rmask")
    mu6, r_mu6 = sb([128, 25, 6], F32, "mu6")
    muf, r_muf = sb([128, 25], F32, "muf")
    om, r_om = sb([128, 25], F32, "om")
    lnxg, r_lnxg = sb([128, 128], F32, "lnxg")
    lnxb, r_lnxb = sb([128, 128], F32, "lnxb")
    wup32, r_wup32 = sb([128, 2, 1024], F32, "wup32")
    wupb, r_wupb = sb([128, 2, 1024], BF16, "wupb")
    ld(rmask[:], rmask_d[:, 0:512], r_rmask)
    ld(mu6[:], mu6_d[:, :, :], r_mu6)
    ld(muf[:], muf_d[:, :], r_muf)
    ld(wup32[0:64, :, :], wup_d[:, :, :], r_wup32)
    ld(wup32[64:128, :, :], aup_d[:, :, :], r_wup32)
    cp("dve", wupb[:], wup32[:], (r_wup32,), (r_wupb,))
    ts("dve", om[:], muf[:], -1.0, 1.0, ALU.mult, ALU.add, (r_muf,), (r_om,))

    raw, r_raw = sb([128, NT], BF16, "raw")
    rsh, r_rsh = sb([128, NT], BF16, "rsh")
    ksh, r_ksh = sb([128, NT], BF16, "ksh")
    vsh, r_vsh = sb([128, NT], BF16, "vsh")
    wdad, r_wdad = sb([128, NT], BF16, "wdad")
    kk, r_kk = sb([128, NT], BF16, "kk")
    rkb, r_rkb = raw, r_raw
    grT, r_grT = sb([128, NT], BF16, "grT")
    zrow, r_zrow = sb([128, NT], BF16, "zrow")
    Vtok, r_Vtok = sb([128, NCHUNK, 128], BF16, "Vtok")
    sbon, r_sbon = sb([128, NCHUNK, 2], F32, "sbon")
    yf, r_yf = sb([128, NCHUNK, 128], F32, "yf")
    t_asig, r_asig = sb([128, 512], F32, "t_asig")
    t_sg, r_sg = sb([128, 512], F32, "t_sg")
    t_cs, r_cs = sb([128, 512], F32, "t_cs")
    t_p, r_p = sb([128, 512], F32, "t_p")
    t_e, r_e = sb([128, 512], F32, "t_e")
    ARs = [sb([128, 4, 2, 128], BF16, "AR%d" % i_) for i_ in range(2)]
    AR, r_AR = ARs[0]
    bt, r_bt = sb([128, 512], BF16, "bt")
    kt, r_kt = sb([128, 512], BF16, "kt")
    Btoks = [sb([128, 4, 128], BF16, "Btok%d" % i_) for i_ in range(2)]
    Btok, r_Btok = Btoks[0]
    Ktoks = [sb([128, 4, 128], BF16, "Ktok%d" % i_) for i_ in range(2)]
    Ktok, r_Ktok = Ktoks[0]
    gams = [sb([128, 8], F32, "gam%d" % i_) for i_ in range(2)]
    gam, r_gam = gams[0]
    NU = 8
    ATu = [sb([128, 512], BF16, "AT%d" % u) for u in range(NU)]
    NP_ = 4
    WSB = [sb([128, 256], BF16, "wsb%d" % i) for i in range(2)]
    ASB = [sb([128, 512], BF16, "asb%d" % i) for i in range(2)]
    XNp = [[sb([128, 512], BF16, "XN%d_%d" % (u, i)) for i in range(2)] for u in range(NP_)]
    PQp = [[sb([128, 512], BF16, "PQ%d_%d" % (u, i)) for i in range(2)] for u in range(NP_)]
    DNp = [sb([128, 512], BF16, "DN%d" % u) for u in range(NP_)]
    O1p = [sb([128, 512], BF16, "O1_%d" % u) for u in range(NP_)]
    O2p = [sb([128, 512], BF16, "O2_%d" % u) for u in range(NP_)]
    YZp = [sb([128, 512], BF16, "YZ%d" % u) for u in range(NP_)]
    Ttp = [sb([128, 2, 128], BF16, "Tt%d" % u) for u in range(NP_)]
    MD, r_MD = sb([128, 2, 256], BF16, "MD")
    MO1, r_MO1 = sb([128, 2, 256], BF16, "MO1")
    MO2, r_MO2 = sb([128, 2, 256], BF16, "MO2")
    I2, r_I2 = sb([128, 512], BF16, "I2")
    ld(I2[:], I2_d[:, :], r_I2)
    for d_ in range(2):
        ld(MD[:, d_, :], mblk_d[0, d_, :, :], r_MD)
        ld(MO1[:, d_, :], mblk_d[1, d_, :, :], r_MO1)
        ld(MO2[:, d_, :], mblk_d[2, d_, :, :], r_MO2)
    ucnt = [0]
    Hf, r_Hf = sb([128, 128], F32, "Hf")
    H1, r_H1 = sb([128, 128], F32, "H1")
    Hm, r_Hm = sb([128, 128], F32, "Hm")
    Hb, r_Hb = sb([128, 128], BF16, "Hb")
    X0b, r_X0b = sb([128, 128], BF16, "X0b")
    Ub, r_Ub = sb([128, 128], BF16, "Ub")
    YS = [sb([128, 128], F32, "ysum%d" % i_) for i_ in range(2)]
    YQ = [sb([128, 128], F32, "ysq%d" % i_) for i_ in range(2)]
    YN = [sb([128, 128], F32, "yn%d" % i_) for i_ in range(2)]
    ZP = [sb([128, 128], BF16, "zpre%d" % i_) for i_ in range(2)]
    ST4 = [sb([128, 8], F32, "st4_%d" % i_) for i_ in range(2)]
    tailcnt = [0]
    pending = []
    tmpa, r_tmpa = t_asig, r_asig
    tmpb, r_tmpb = bt, r_bt
    tmpc, r_tmpc = t_sg, r_sg
    rr["n"] = 4
    rS = [Res() for _ in range(4)]

    def shift_rows(eng, dst, rdst, tile):
        ts(eng, dst[:], raw[:], om[:, tile:tile + 1], None, ALU.mult, ALU.bypass, (r_raw, r_om), (rdst,))
        d3 = dst[:, 0:4096].rearrange("p (r w) -> p r w", w=64)
        s3 = raw[:, 0:4096].rearrange("p (r w) -> p r w", w=64)
        pairs = [(d3[:, :, 1:64], s3[:, :, 0:63], 0), (d3[:, :, 0:63], s3[:, :, 1:64], 1),
                 (d3[:, 1:64, :], s3[:, 0:63, :], 2), (d3[:, 0:63, :], s3[:, 1:64, :], 3)]
        dc = dst[:, 4096:NT].rearrange("p (s t) -> p s t", t=256)
        sc = raw[:, 4096:NT].rearrange("p (s t) -> p s t", t=256)
        pairs += [(dc[:, :, 1:256], sc[:, :, 0:255], 4), (dc[:, :, 0:255], sc[:, :, 1:256], 5)]
        for o_, i_, m in pairs:
            stt(eng, o_, i_, mu6[:, tile, m:m + 1], o_, ALU.mult, ALU.add, (r_raw, r_mu6, rdst), (rdst,))

    ld(raw[:], UT[32, :, :], r_raw)
    shift_rows("dve", wdad, r_wdad, 24)
    act(wdad[0:64, :], wdad[0:64, :], AF.Tanh, (r_wdad,), (r_wdad,))
    if stop3 == 1:
        dump(0, wdad[:], r_wdad, NT)
        return finish()

    SEGS = [(4 * i_, 4) for i_ in range(9)]

    for hp in range(nhp):
        hc = slice(hp * 128, hp * 128 + 128)
        for (tile, dst, rdst, eng) in ((hp, rsh, r_rsh, "dve"), (8 + hp, ksh, r_ksh, "pool"), (16 + hp, vsh, r_vsh, "dve")):
            ld(raw[:], UT[8 + tile, :, :], r_raw)
            shift_rows(eng, dst, rdst, tile)
        ld(grT[:], UT[33 + hp, :, :], r_grT, q="pool")
        ld(lnxg[:], lnxg_d[:, hc], r_lnxg, q="pool")
        ld(lnxb[:], lnxb_d[:, hc], r_lnxb, q="pool")
        for blk in range(9):
            cs_ = slice(blk * 512, blk * 512 + 512)
            ts("dve", tmpa[:], ksh[:, cs_], pcols[:, 0, hp:hp + 1], None, ALU.mult, ALU.bypass, (r_ksh, r_pcols), (r_tmpa,))
            act(tmpb[:], tmpa[:], AF.Square, (r_tmpa,), (r_tmpb,))
            p_, rp = nps()
            mm(p_[:, :], bones[:], tmpb[:], True, True, (r_bones, r_tmpb), (rp,))
            act(tmpc[:], p_[:, :], AF.Sqrt, (rp,), (r_tmpc,))
            ts("dve", tmpc[:], tmpc[:], 1e-12, None, ALU.max, ALU.bypass, (r_tmpc,), (r_tmpc,))
            g.op("dve", lambda e: e.reciprocal(tmpc[:], tmpc[:]), (r_tmpc,), (r_tmpc,))
            tt("dve", kk[:, cs_], tmpa[:], tmpc[:], ALU.mult, (r_tmpa, r_tmpc), (r_kk,))
        stt("pool", rkb[:], rsh[:], pcols[:, 2, hp:hp + 1], ksh[:], ALU.mult, ALU.mult, (r_rsh, r_pcols, r_ksh), (r_rkb,))
        for c in range(NCHUNK):
            cc_ = slice(c * 128, c * 128 + 128)
            p_, rp = npt()
            tr(p_[:, 0:128], vsh[:, cc_], ident[:], (r_vsh, r_ident), (rp,))
            cp("act", Vtok[:, c, :], p_[:, 0:128], (rp,), (r_Vtok,))
            q_, rq = nps()
            mm(q_[:, 0:2], rkb[:, cc_], bones2[:], True, True, (r_rkb, r_bones2), (rq,))
            cp("dve", sbon[:, c, :], q_[:, 0:2], (rq,), (r_sbon,))

        if stop3 == 2:
            for i_, (t_, r_) in enumerate(((rsh, r_rsh), (ksh, r_ksh), (vsh, r_vsh), (kk, r_kk))):
                dump(i_ * NT, t_[:], r_, NT)
            dump(4 * NT, Vtok[:].rearrange("p c t -> p (c t)"), r_Vtok, NT)
            dump(0, sbon[:].rearrange("p c t -> p (c t)"), r_sbon, 72, f32=True)
            return finish()
        for d in range(2):
            segs = SEGS[0:8] if d == 0 else SEGS[7::-1]
            segs = list(segs) + [SEGS[8]]
            def derived(ch0, nch, par):
                AR, r_AR = ARs[par]
                Btok, r_Btok = Btoks[par]
                Ktok, r_Ktok = Ktoks[par]
                gam, r_gam = gams[par]
                col0 = ch0 * 128
                ncol = nch * 128
                sc_ = slice(col0, col0 + ncol)
                for b0 in range(0, ncol, 512):
                    bs = slice(col0 + b0, col0 + b0 + 512)
                    ls = slice(b0, b0 + 512)
                    p_, rp = nps()
                    mm(p_[:, :], wupb[64:128, d, hc], wdad[64:128, bs], True, True, (r_wupb, r_wdad), (rp,))
                    act(t_asig[:, ls], p_[:, :], AF.Sigmoid, (rp, r_pcols), (r_asig,), bias=pcols[:, 5 + d, hp:hp + 1])
                    p_, rp = nps()
                    mm(p_[:, :], wupb[0:64, d, hc], wdad[0:64, bs], True, True, (r_wupb, r_wdad), (rp,))
                    act(t_sg[:, ls], p_[:, :], AF.Sigmoid, (rp, r_pcols), (r_sg,), bias=pcols[:, 3 + d, hp:hp + 1])
                L = slice(0, ncol)
                yield
                g.op("dve", lambda e: e.tensor_tensor_scan(t_cs[:, L], rmask[:, L], t_sg[:, L], 0.0, ALU.mult, ALU.add),
                     (r_rmask, r_sg), (r_cs,))
                yield
                cs3 = t_cs[:, L].rearrange("p (c t) -> p c t", t=128)
                act(gam[:, 0:nch], cs3[:, :, 127], AF.Exp, (r_cs,), (r_gam,), scale=-C0)
                yield
                v3 = lambda a: a.rearrange("p (c t) -> p c t", t=128)
                if d == 0:
                    tt("pool", t_p[:, L], t_cs[:, L], t_sg[:, L], ALU.subtract, (r_cs, r_sg), (r_p,))
                    pin, rpin = t_cs, r_cs
                else:
                    for ci in range(nch):
                        l1 = slice(ci * 128, ci * 128 + 128)
                        ts("pool", t_p[:, l1], t_cs[:, l1], -1.0, t_cs[:, ci * 128 + 127:ci * 128 + 128], ALU.mult, ALU.add,
                           (r_cs,), (r_p,))
                pex, rpex = t_p, r_p
                act(t_e[:, L], pex[:, L], AF.Exp, (rpex,), (r_e,), scale=-C0)
                stt("dve", AR[:, 0:nch, 0, :], v3(kk[:, sc_]), -1.0, v3(t_e[:, L]), ALU.mult, ALU.mult, (r_kk, r_e), (r_AR,))
                yield
                if d == 1:
                    tt("pool", t_p[:, L], t_p[:, L], t_sg[:, L], ALU.add, (r_p, r_sg), (r_p,))
                    pin, rpin = t_p, r_p
                act(t_e[:, L], pin[:, L], AF.Exp, (rpin,), (r_e,), scale=-C0)
                yield
                tt("dve", AR[:, 0:nch, 1, :], v3(rsh[:, sc_]), v3(t_e[:, L]), ALU.mult, (r_rsh, r_e), (r_AR,))
                yield
                act(t_e[:, L], pin[:, L], AF.Exp, (rpin,), (r_e,), scale=C0)
                tt("dve", t_sg[:, L], kk[:, sc_], t_asig[:, L], ALU.mult, (r_kk, r_asig), (r_sg,))
                tt("dve", bt[:, L], t_sg[:, L], t_e[:, L], ALU.mult, (r_sg, r_e), (r_bt,))
                yield
                ts("pool", t_asig[:, L], t_asig[:, L], pcols[:, 1, hp:hp + 1], oka[:, hp:hp + 1], ALU.mult, ALU.add,
                   (r_asig, r_pcols, r_oka), (r_asig,))
                tt("pool", t_asig[:, L], t_asig[:, L], ksh[:, sc_], ALU.mult, (r_asig, r_ksh), (r_asig,))
                yield
                tt("dve", kt[:, L], t_asig[:, L], t_e[:, L], ALU.mult, (r_asig, r_e), (r_kt,))
                yield "T"
                for ci in range(nch):
                    l1 = slice(ci * 128, ci * 128 + 128)
                    p_, rp = npt()
                    tr(p_[:, 0:128], bt[:, l1], ident[:], (r_bt, r_ident), (rp,))
                    cp("act", Btok[:, ci, :], p_[:, 0:128], (rp,), (r_Btok,))
                    p_, rp = npt()
                    tr(p_[:, 0:128], kt[:, l1], ident[:], (r_kt, r_ident), (rp,))
                    cp("dve", Ktok[:, ci, :], p_[:, 0:128], (rp,), (r_Ktok,))

            gen0 = derived(segs[0][0], segs[0][1], 0)
            for _ in gen0:
                pass
            for si, (ch0, nch) in enumerate(segs):
                par = si % 2
                AR, r_AR = ARs[par]
                Btok, r_Btok = Btoks[par]
                Ktok, r_Ktok = Ktoks[par]
                gam, r_gam = gams[par]
                nxt = derived(segs[si + 1][0], segs[si + 1][1], 1 - par) if si + 1 < len(segs) else iter(())
                atT = [False]

                def adv(nxt=nxt, atT=atT):
                    if not atT[0]:
                        if next(nxt, None) == "T":
                            atT[0] = True
                v2 = lambda ap_: ap_.rearrange("p (h c) -> p h c", h=2)
                for ci in range(nch):
                    l1 = slice(ci * 128, ci * 128 + 128)
                    dn, rdn = DNp[ci]
                    for h in range(2):
                        u = ci * 2 + h
                        hs = slice(64 * h, 64 * h + 64)
                        AT, rAT = ATu[u]
                        p_, rp = nps()
                        arv = AR[hs, ci, :, :].rearrange("p a t -> p (a t)")
                        mm(p_[:, 0:256], bt[hs, l1], arv, True, True, (r_bt, r_AR), (rp,))
                        mm(p_[:, 256:512], kt[hs, l1], arv, True, True, (r_kt, r_AR), (rp,))
                        asb, rasb = ASB[u % 2]
                        cp("act", asb[:], p_[:, :], (rp,), (rasb,))
                        tt("dve", AT[:], asb[:], mask4[:, d, :], ALU.mult, (rasb, r_mask4), (rAT,))
                        w_, rw = nps()
                        mm(w_[:, 0:128], AR[hs, ci, 0, :], bt[hs, l1], True, True, (r_AR, r_bt), (rw,))
                        mm(w_[:, 128:256], bt[hs, l1], AR[hs, ci, 0, :], True, True, (r_AR, r_bt), (rw,))
                        o = 256 * h
                        wsb, rwsb = WSB[u % 2]
                        cp("act", wsb[:], w_[:, 0:256], (rw,), (rwsb,))
                        tt("dve", dn[:, o:o + 256], wsb[:], MD[:, d, 0:256], ALU.mult, (rwsb, r_MD), (rdn,))
                        tt("dve", O1p[ci][0][:, o:o + 256], wsb[:], MO1[:, d, 0:256], ALU.mult, (rwsb, r_MO1), (O1p[ci][1],))
                        tt("dve", O2p[ci][0][:, o:o + 256], wsb[:], MO2[:, d, 0:256], ALU.mult, (rwsb, r_MO2), (O2p[ci][1],))
                    pq, rpq = PQp[ci][0]
                    tt("dve", pq[:], dn[:], I2[:], ALU.add, (rdn, r_I2), (rpq,))
                for lev in range(4):
                    for ci in range(nch):
                        xn, rxn = DNp[ci] if lev == 0 else XNp[ci][lev % 2]
                        xn2, rxn2 = XNp[ci][(lev + 1) % 2]
                        p2, rp2 = nps()
                        for h in range(2):
                            o = 256 * h
                            mm(p2[:, o:o + 128], xn[:, o + 128:o + 256], xn[:, o:o + 128], True, True, (rxn,), (rp2,))
                            mm(p2[:, o + 128:o + 256], xn[:, o:o + 128], xn[:, o + 128:o + 256], True, True, (rxn,), (rp2,))
                        cp("act", xn2[:], p2[:, :], (rp2,), (rxn2,))
                        adv()
                    for ci in range(nch):
                        xn2, rxn2 = XNp[ci][(lev + 1) % 2]
                        pq, rpq = PQp[ci][lev % 2]
                        pq2, rpq2 = PQp[ci][(lev + 1) % 2]
                        q_, rq = nps()
                        for h in range(2):
                            o = 256 * h
                            mm(q_[:, o:o + 128], xn2[:, o + 128:o + 256], pq[:, o:o + 128], True, True, (rxn2, rpq), (rq,))
                            mm(q_[:, o + 128:o + 256], xn2[:, o:o + 128], pq[:, o + 128:o + 256], True, True, (rxn2, rpq), (rq,))
                        tt("dve", pq2[:], q_[:, :], pq[:], ALU.add, (rq, rpq), (rpq2,))
                        adv()
                for ci in range(nch):
                    pq, rpq = PQp[ci][0]
                    o1, ro1 = O1p[ci]
                    yz, ryz = YZp[ci]
                    p2, rp2 = nps()
                    for h in range(2):
                        o = 256 * h
                        mm(p2[:, o:o + 128], o1[:, o + 128:o + 256], pq[:, o:o + 128], True, True, (ro1, rpq), (rp2,))
                        mm(p2[:, o + 128:o + 256], o1[:, o:o + 128], pq[:, o + 128:o + 256], True, True, (ro1, rpq), (rp2,))
                    cp("act", yz[:], p2[:, :], (rp2,), (ryz,))
                for ci in range(nch):
                    pq, rpq = PQp[ci][0]
                    pq2, rpq2 = PQp[ci][1]
                    yz, ryz = YZp[ci]
                    q_, rq = nps()
                    for h in range(2):
                        o = 256 * h
                        mm(q_[:, o:o + 128], pq[:, o + 128:o + 256], yz[:, o:o + 128], True, True, (rpq, ryz), (rq,))
                        mm(q_[:, o + 128:o + 256], pq[:, o:o + 128], yz[:, o + 128:o + 256], True, True, (rpq, ryz), (rq,))
                    tt("dve", pq2[:], q_[:, :], pq[:], ALU.add, (rq, rpq), (rpq2,))
                for ci in range(nch):
                    pq, rpq = PQp[ci][1]
                    o2, ro2 = O2p[ci]
                    yz, ryz = YZp[ci]
                    p2, rp2 = nps()
                    for h in range(2):
                        o = 256 * h
                        mm(p2[:, o:o + 128], o2[:, o:o + 128], pq[:, o + 128:o + 256], True, True, (ro2, rpq), (rp2,))
                    cp("act", v2(yz[:])[:, :, 0:128], v2(p2[:, :])[:, :, 0:128], (rp2,), (ryz,))
                for ci in range(nch):
                    pq, rpq = PQp[ci][1]
                    yz, ryz = YZp[ci]
                    q_, rq = nps()
                    for h in range(2):
                        o = 256 * h
                        mm(q_[:, o:o + 128], pq[:, o:o + 128], yz[:, o:o + 128], True, True, (rpq, ryz), (rq,))
                    tt("dve", Ttp[ci][0][:], v2(q_[:, :])[:, :, 0:128], v2(pq[:])[:, :, 128:256], ALU.add, (rq, rpq), (Ttp[ci][1],))
                for _ in nxt:
                    pass
                if ch0 < 32:
                    order = list(range(nch)) if d == 0 else list(range(nch - 1, -1, -1))
                    seqs = [(-1, order)]
                else:
                    seqs = [(0, [0, 1] if d == 0 else [1, 0]), (1, [2, 3] if d == 0 else [3, 2])]
                for (sq, order) in seqs:
                    for oi, ci in enumerate(order):
                        gc = ch0 + ci
                        l1 = slice(ci * 128, ci * 128 + 128)
                        first_lat = (sq == -1 and oi == 0 and ch0 == (0 if d == 0 else 28))
                        if first_lat:
                            ld(Hf[:], h0_d[d, hp, :, :], r_Hf)
                            cp("act", Hb[:], Hf[:], (r_Hf,), (r_Hb,))
                        elif sq >= 0 and oi == 0:
                            g.op("dve", lambda e: e.memset(Hf[:], 0.0), (), (r_Hf,))
                            g.op("pool", lambda e: e.memset(Hb[:], 0.0), (), (r_Hb,))
                        Tt = [(Ttp[ci][0][:, 0, :], Ttp[ci][1]), (Ttp[ci][0][:, 1, :], Ttp[ci][1])]
                        ATb = [ATu[ci * 2], ATu[ci * 2 + 1]]
                        ucnt[0] += 1
                        psXU, psY = ps[4][0], ps[5][0]
                        X0p, Up, Yp, Pp = (psXU[:, 0:128], psXU[:, 128:256], psY[:, 0:128], psY[:, 256:384])
                        mm(X0p, AR[:, ci, 0, :], Hb[:], True, False, (r_AR, r_Hb), (rS[0],))
                        for h in range(2):
                            hcs = slice(64 * h, 64 * h + 64)
                            mm(psXU[:, hcs], ATb[h][0][:, 256:384], Vtok[:, gc, hcs], False, h == 1, (ATb[h][1], r_Vtok), (rS[0],))
                        cp("act", X0b[:], X0p, (rS[0],), (r_X0b,))
                        for h in range(2):
                            mm(psXU[:, 128 + 64 * h:192 + 64 * h], Tt[h][0], X0b[:, 64 * h:64 * h + 64], True, True,
                               (Tt[h][1], r_X0b), (rS[1],))
                        cp("act", Ub[:], Up, (rS[1],), (r_Ub,))
                        mm(Yp, AR[:, ci, 1, :], Hb[:], True, False, (r_AR, r_Hb), (rS[2],))
                        for h in range(2):
                            hcs = slice(64 * h, 64 * h + 64)
                            yo = psY[:, 64 * h:64 * h + 64]
                            mm(yo, ATb[h][0][:, 128:256], Ub[:, hcs], False, False, (ATb[h][1], r_Ub), (rS[2],))
                            mm(yo, ATb[h][0][:, 384:512], Vtok[:, gc, hcs], False, h == 1, (ATb[h][1], r_Vtok), (rS[2],))
                        mm(Pp, Btok[:, ci, :], Ub[:], True, False, (r_Btok, r_Ub), (rS[3],))
                        mm(Pp, Ktok[:, ci, :], Vtok[:, gc, :], False, True, (r_Ktok, r_Vtok), (rS[3],))
                        stt("dve", Hm[:], Pp, gam[:, ci:ci + 1], bdm[:], ALU.mult, ALU.mult, (rS[3], r_gam, r_bdm), (r_Hm,))
                        stt("dve", Hb[:], Hf[:], gam[:, ci:ci + 1], Hm[:], ALU.mult, ALU.add, (r_Hf, r_gam, r_Hm), (r_Hb,))
                        stt("dve", Hf[:], Hf[:], gam[:, ci:ci + 1], Hm[:], ALU.mult, ALU.add, (r_Hf, r_gam, r_Hm), (r_Hf,))
                        if sq >= 0 and oi == len(order) - 1:
                            g.dma("sp", hst[sq, d, hp, :, :], Hf[:], (r_Hf,), ())
                        if d == 0:
                            cp("dve", yf[:, gc, :], Yp, (rS[2],), (r_yf,))
                            continue
                        tp = tailcnt[0] % 2
                        tailcnt[0] += 1
                        ys_, rys_ = YS[tp]
                        tt("dve", ys_[:], Yp, yf[:, gc, :], ALU.add, (rS[2], r_yf), (rys_,))
                        if len(pending) == 2:
                            for _ in pending.pop(0):
                                pass
                        if pending:
                            next(pending[-1])

                        def tail(tp=tp, gc=gc):
                            ys_, rys_ = YS[tp]
                            yq_, ryq_ = YQ[tp]
                            yn_, ryn_ = YN[tp]
                            zp_, rzp_ = ZP[tp]
                            s4, rs4 = ST4[tp]
                            g.op("dve", lambda e: e.reduce_sum(s4[:, 0:2], ys_[:].rearrange("p (h i) -> p h i", h=2), AX.X),
                                 (rys_,), (rs4,))
                            tt("dve", yq_[:], ys_[:], ys_[:], ALU.mult, (rys_,), (ryq_,))
                            g.op("dve", lambda e: e.reduce_sum(s4[:, 2:4], yq_[:].rearrange("p (h i) -> p h i", h=2), AX.X),
                                 (ryq_,), (rs4,))
                            ts("dve", s4[:, 0:4], s4[:, 0:4], 1.0 / 64, None, ALU.mult, ALU.bypass, (rs4,), (rs4,))
                            tt("dve", s4[:, 4:6], s4[:, 0:2], s4[:, 0:2], ALU.mult, (rs4,), (rs4,))
                            tt("dve", s4[:, 4:6], s4[:, 2:4], s4[:, 4:6], ALU.subtract, (rs4,), (rs4,))
                            act(s4[:, 6:8], s4[:, 4:6], AF.Sqrt, (rs4,), (rs4,), bias=GN_EPS)
                            g.op("dve", lambda e: e.reciprocal(s4[:, 6:8], s4[:, 6:8]), (rs4,), (rs4,))
                            for h in range(2):
                                hcs = slice(64 * h, 64 * h + 64)
                                ts("dve", yn_[:, hcs], ys_[:, hcs], s4[:, h:h + 1], s4[:, 6 + h:7 + h], ALU.subtract, ALU.mult,
                                   (rys_, rs4), (ryn_,))
                            tt("pool", yn_[:], yn_[:], lnxg[:], ALU.mult, (ryn_, r_lnxg), (ryn_,))
                            tt("pool", yn_[:], yn_[:], lnxb[:], ALU.add, (ryn_, r_lnxb), (ryn_,))
                            for h in range(2):
                                hcs = slice(64 * h, 64 * h + 64)
                                ts("pool", yq_[:, hcs], Vtok[:, gc, hcs], sbon[:, gc, h:h + 1], None, ALU.mult, ALU.bypass,
                                   (r_Vtok, r_sbon), (ryq_,))
                            tt("pool", zp_[:], yq_[:], yn_[:], ALU.add, (ryq_, ryn_), (rzp_,))
                            yield
                            p_, rp = npt()
                            tr(p_[:, 0:128], zp_[:], ident[:], (rzp_, r_ident), (rp,))
                            gcs = slice(gc * 128, gc * 128 + 128)
                            tt("dve", zrow[:, gcs], p_[:, 0:128], grT[:, gcs], ALU.mult, (rp, r_grT), (r_zrow,))
                        pending.append(tail())
                for _ in nxt:
                    pass
        while pending:
            for _ in pending.pop(0):
                pass
        g.dma("sp", ZT[hp, :, :], zrow[:], (r_zrow,), ())

    if stop == 4:
        return finish()
    g.barrier()
    arena["off"] = PERSIST
    rr["n"] = 4
    wpf, r_wpf = sb([128, 4, 1024], BF16, "wpf")
    wpr, r_wpr = sb([128, 8, 1024], BF16, "wpr")
    wo, r_wo = sb([128, 8, 1024], BF16, "wo")
    wst = [sb([128, 1024], F32, "wst%d" % i) for i in range(2)]
    n_ = 0
    for (dst, rdst, src, nk) in ((wpf, r_wpf, wpf_d, 4), (wpr, r_wpr, wpr_d, 8), (wo, r_wo, wo_d, 8)):
        for kc in range(nk):
            w_, rw = wst[n_ % 2]
            ld(w_[:], src[kc * 128:(kc + 1) * 128, :], rw, q="pool" if n_ % 2 else "sp")
            cp("pool" if n_ % 2 else "dve", dst[:, kc, :], w_[:], (rw,), (rdst,))
            n_ += 1
    FZb, r_FZb = sb([128, 4, 512], BF16, "FZb")
    ZTb, r_ZTb = sb([128, 8, 512], BF16, "ZTb")
    MG, r_MG = sb([128, 16, 512], BF16, "MG")
    mT, r_mT = sb([128, 8, 512], BF16, "mT")
    u1, r_u1 = sb([128, 512], F32, "u1")
    u2, r_u2 = sb([128, 512], F32, "u2")
    xt5 = [sb([128, 1024], F32, "xt5_%d" % i) for i in range(2)]
    o1, r_o1 = sb([128, 1024], F32, "o1")
    xs, r_xs = sb([128, 1024], F32, "xs")
    yo_ = [sb([128, 1024], F32, "yo%d" % i) for i in range(2)]
    junk5, r_junk5 = sb([128, 1024], BF16, "junk5")
    ss5 = [sb([128, 1], F32, "ss5_%d" % i) for i in range(2)]
    for tb in range(9):
        bs = slice(tb * 512, tb * 512 + 512)
        ld(FZb[:], FZ.ap()[:, :, bs].rearrange("g p t -> p g t"), r_FZb)
        ld(ZTb[:], ZT.ap()[:, :, bs].rearrange("g p t -> p g t"), r_ZTb, q="pool")
        ld(MG[:], UT.ap()[41:57, :, bs].rearrange("g p t -> p g t"), r_MG)
        for n in range(8):
            ns = slice(n * 128, n * 128 + 128)
            pf, rpf = nps()
            for kc in range(4):
                mm(pf[:, :], wpf[:, kc, ns], FZb[:, kc, :], kc == 0, kc == 3, (r_wpf, r_FZb), (rpf,))
            pr, rpr = nps()
            for kc in range(8):
                mm(pr[:, :], wpr[:, kc, ns], ZTb[:, kc, :], kc == 0, kc == 7, (r_wpr, r_ZTb), (rpr,))
            tt("dve", u1[:], pf[:, :], MG[:, n, :], ALU.mult, (rpf, r_MG), (r_u1,))
            tt("dve", u2[:], pr[:, :], MG[:, 8 + n, :], ALU.mult, (rpr, r_MG), (r_u2,))
            tt("pool", mT[:, n, :], u1[:], u2[:], ALU.add, (r_u1, r_u2), (r_mT,))
        for t4 in range(4):
            i = tb * 4 + t4
            which = 0 if i < 32 else 1
            x_, rx = xt5[i % 2]
            ld(x_[:], x_all[i * 128:(i + 1) * 128, :], rx, q="pool")
            for half in range(2):
                hs = slice(half * 512, half * 512 + 512)
                po, rpo = nps()
                for kc in range(8):
                    mm(po[:, :], mT[:, kc, t4 * 128:(t4 + 1) * 128], wo[:, kc, hs], kc == 0, kc == 7, (r_mT, r_wo), (rpo,))
                tt("dve", o1[:, hs], po[:, :], Gbc[:, which, hs], ALU.mult, (rpo, r_Gbc), (r_o1,))
            tt("pool", xs[:], o1[:], x_[:], ALU.add, (r_o1, rx), (r_xs,))
            s_, rs_ = ss5[i % 2]
            act(junk5[:], xs[:], AF.Square, (r_xs,), (r_junk5, rs_), accum=s_[:])
            act(s_[:], s_[:], AF.Sqrt, (rs_,), (rs_,), bias=RMS_EPS, scale=1.0 / 1024)
            g.op("dve", lambda e: e.reciprocal(s_[:], s_[:]), (rs_,), (rs_,))
            y_, ry = yo_[i % 2]
            stt("dve", y_[:], xs[:], s_[:, 0:1], fgb[:], ALU.mult, ALU.mult, (r_xs, rs_, r_fgb), (ry,))
            g.dma("sp", y_all[i * 128:(i + 1) * 128, :], y_[:], (ry,), ())
    _CACHE['cnt'] = (dict(g.cnt), list(g.dcnt))
    for i in range(len(g.dcnt)):
        if g.dcnt[i]:
            nc.sync.wait_ge(g.dsem[i], g.dcnt[i])
    return nc


_CACHE = {}


def _consts():
    if "c" in _CACHE:
        return _CACHE["c"]
    c = {}
    c["ident"] = np.eye(128, dtype=np.float32).astype(NPBF)
    bo = np.zeros((128, 128), np.float32)
    bo[:64, :64] = 1
    bo[64:, 64:] = 1
    c["bones"] = bo.astype(NPBF)
    c["bdm"] = bo.copy()
    b2 = np.zeros((128, 2), np.float32)
    b2[:64, 0] = 1
    b2[64:, 1] = 1
    c["bones2"] = b2.astype(NPBF)
    s = np.arange(128)[:, None]
    t = np.arange(128)[None, :]
    m4 = np.zeros((2, 128, 512), np.float32)
    for d, (st, inc) in enumerate((((s < t), (s <= t)), ((s > t), (s >= t)))):
        m4[d, :, 0:128] = st
        m4[d, :, 128:256] = inc
        m4[d, :, 256:384] = st
        m4[d, :, 384:512] = inc
    c["mask4"] = m4.astype(NPBF)
    mT = np.zeros((2, 128, 128), np.float32)
    mT[0] = (t < s)
    mT[1] = (t > s)
    c["maskT"] = mT.astype(NPBF)
    c["I2"] = np.concatenate([np.eye(128)] * 4, 1).astype(np.float32).astype(NPBF)
    rI = np.arange(128)[:, None]
    cI = np.arange(128)[None, :]
    mblk = np.zeros((3, 2, 128, 256), np.float32)
    for d in range(2):
        for half in range(2):
            tI, sI = (rI, cI) if half == 0 else (cI, rI)
            strict = (sI < tI) if d == 0 else (sI > tI)
            b32 = (tI // 32) == (sI // 32)
            b64 = (tI // 64) == (sI // 64)
            hs_ = slice(half * 128, half * 128 + 128)
            mblk[0, d, :, hs_] = strict & b32
            mblk[1, d, :, hs_] = strict & b64 & ~b32
            mblk[2, d, :, hs_] = strict & ~b64
    c["mblk"] = mblk.astype(NPBF)
    rm = np.ones((128, 1024), np.float32)
    rm[:, ::128] = 0
    c["rmask"] = rm
    sw = np.zeros((2, 2, 128), np.float32)
    sw[0, 0, :] = 1
    sw[1, 1, :] = 1
    c["selw"] = sw
    n = 4096
    tt_ = np.arange(n, dtype=np.int64)
    ph = (tt_[:, None] * tt_[None, :]) % n
    ang = 2 * np.pi * ph.astype(np.float64) / n
    sc = 1.0 / np.sqrt(n * 128.0)
    tabs = np.stack([np.cos(ang) * sc, np.sin(ang) * sc], 0).astype(np.float32)
    tabs = tabs.reshape(2, 32, 128, 8, 512).transpose(0, 3, 2, 1, 4)
    c["dftL"] = np.ascontiguousarray(tabs).reshape(2, 8, 128, 32 * 512).astype(NPBF)
    n2 = 256
    t2 = np.arange(n2, dtype=np.int64)
    ang2 = 2 * np.pi * ((t2[:, None] * t2[None, :]) % n2).astype(np.float64) / n2
    sc2 = 1.0 / np.sqrt(n2 * 128.0)
    tb2 = np.stack([np.cos(ang2) * sc2, np.sin(ang2) * sc2], 0).astype(np.float32)
    tb2 = tb2.reshape(2, 2, 128, 256).transpose(2, 0, 1, 3)
    c["dftC"] = np.ascontiguousarray(tb2).astype(NPBF)
    cidx = np.arange(128, dtype=np.int64)
    ang3 = 2 * np.pi * ((cidx[:, None] * cidx[None, :]) % 128).astype(np.float64) / 128
    c128 = np.stack([np.cos(ang3), -np.sin(ang3)], 1).astype(np.float32)
    c["c128"] = np.ascontiguousarray(c128).astype(NPBF)
    _CACHE["c"] = c
    return c


def kernel(x_prompt, x_sample, state_rwkv, c, c_ctx, norm_g, w_ada, b_ada, w_in, mu_shift, w0, w_up, a0, a_up,
           k_k, k_a, r_k, lnx_g, lnx_b, w_proj_f, w_proj_r, w_out, final_g):
    f = lambda a: np.ascontiguousarray(np.asarray(a, dtype=np.float32))
    x_prompt, x_sample, state_rwkv, c, c_ctx = f(x_prompt), f(x_sample), f(state_rwkv), f(c), f(c_ctx)
    cst = _consts()
    col = lambda v: np.ascontiguousarray(f(v).reshape(8, 128).T)
    pcols = np.stack([col(k_k[0]), col(k_a[0]), col(r_k[0].reshape(-1)), col(w0[0, 0]), col(w0[0, 1]),
                      col(a0[0, 0]), col(a0[0, 1])], 1)
    mu = f(mu_shift[0])
    muf = np.ascontiguousarray(mu.reshape(25, 128).T)
    sidx = np.arange(3200).reshape(25, 128).T
    mu6 = np.zeros((128, 25, 6), np.float32)
    for m in range(4):
        mu6[:, :, m] = np.where(sidx % 4 == m, muf, 0)
    mu6[:, :, 4] = np.where(sidx % 2 == 0, muf, 0)
    mu6[:, :, 5] = np.where(sidx % 2 == 1, muf, 0)
    rep = lambda v: np.ascontiguousarray(np.broadcast_to(f(v).reshape(1, -1), (128, f(v).size)))
    shared = dict(cst)
    shared.update({
        "w_ada": f(w_ada[0]), "b_ada2": np.ascontiguousarray(np.broadcast_to(f(b_ada[0])[None], (2, 3072))),
        "ngb": rep(norm_g[0]), "fgb": rep(final_g), "w_in": f(w_in[0]), "pcols": np.ascontiguousarray(pcols),
        "mu6": mu6, "muf": muf, "lnxg": rep(lnx_g[0]), "lnxb": rep(lnx_b[0]),
        "wup": np.ascontiguousarray(f(w_up[0]).transpose(1, 0, 2)), "aup": np.ascontiguousarray(f(a_up[0]).transpose(1, 0, 2)),
        "w_proj_f": f(w_proj_f[0]), "w_proj_r": f(w_proj_r[0]), "w_out": f(w_out[0]),
    })
    in_maps = []
    for core in range(8):
        b = core // 4
        m = dict(shared)
        m["x_all"] = np.ascontiguousarray(np.concatenate([x_sample[b], x_prompt[2 * core], x_prompt[2 * core + 1]], 0))
        ccv = np.stack([c[b].reshape(8, 128).T, c_ctx.reshape(8, 128).T], -1)
        m["cc"] = np.ascontiguousarray(ccv)
        st = state_rwkv[b, 0]
        h0 = np.zeros((2, 8, 128, 128), np.float32)
        for d in range(2):
            for hp in range(8):
                for h in range(2):
                    h0[d, hp, 64 * h:64 * h + 64, 64 * h:64 * h + 64] = st[d, 2 * hp + h].T
        m["h0bd"] = h0
        in_maps.append(m)
    if _CACHE.get("prep_only"):
        return in_maps
    if "nc" not in _CACHE:
        _CACHE["nc"] = build()
    res = run_bass_kernel_spmd(_CACHE["nc"], in_maps, core_ids=list(range(8)))
    y_prompt = np.zeros((16, 256, 1024), np.float32)
    y_sample = np.zeros((2, 4096, 1024), np.float32)
    new_state = np.zeros((16, 1, 2, 16, 64, 64), np.float32)
    for core in range(8):
        r = res.results[core]
        ya = np.asarray(r["y_all"])
        q = core % 4
        y_sample[core // 4, q * 1024:(q + 1) * 1024] = ya[q * 1024:(q + 1) * 1024]
        y_prompt[2 * core] = ya[4096:4352]
        y_prompt[2 * core + 1] = ya[4352:4608]
        hs_ = np.asarray(r["hst"])
        for sq in range(2):
            for d in range(2):
                for hp in range(8):
                    for h in range(2):
                        new_state[2 * core + sq, 0, d, 2 * hp + h] = hs_[sq, d, hp, 64 * h:64 * h + 64, 64 * h:64 * h + 64].T
    return (y_prompt, y_sample, new_state)
```
